# Optimizing a Trainium2 kernel written in Bass

```python
import jax, jax.numpy as jnp
from jax import lax
import numpy as np

D_MODEL = 1024
BATCH = 8
SEQ = 4096
DEPTH = 1

CONV_CH = 1024
CONV_WIDTH = 31
ATTN_GROUPS = ((128, 1), (512, 4), (2048, 16))
HEADS_PER_GROUP = 8
HEAD_DIM = 64
N_ATTN_HEADS = HEADS_PER_GROUP * len(ATTN_GROUPS)
ATTN_WIDTH = N_ATTN_HEADS * HEAD_DIM
SLOT_WIDTH = HEADS_PER_GROUP * HEAD_DIM
N_BRANCH = 2
D_FF = 2816
FFN_CONV_WIDTH = 3
RMS_EPS = 1e-6
LN_EPS = 1e-5
MASK_VALUE = -1e30
SPLIT_SIZES = (CONV_CH, CONV_CH, ATTN_WIDTH, ATTN_WIDTH, ATTN_WIDTH)
IN_WIDTH = sum(SPLIT_SIZES) + N_BRANCH * D_MODEL

kernel_name = "hybrid_conformer_dilated_attn_gated_encoder"


def rms_norm(x, g):
    xf = x.astype(jnp.float32)
    y = xf * lax.rsqrt(jnp.mean(xf * xf, axis=-1, keepdims=True) + RMS_EPS)
    return (y * g.astype(jnp.float32)).astype(x.dtype)


def layer_norm(x, g, b):
    xf = x.astype(jnp.float32)
    mu = jnp.mean(xf, axis=-1, keepdims=True)
    xc = xf - mu
    var = jnp.mean(xc * xc, axis=-1, keepdims=True)
    y = xc * lax.rsqrt(var + LN_EPS) * g.astype(jnp.float32) + b.astype(jnp.float32)
    return y.astype(x.dtype)


def depthwise_conv(x, w, b):
    k = w.shape[0]
    c = x.shape[-1]
    y = lax.conv_general_dilated(
        x, w[:, None, :].astype(x.dtype), window_strides=(1,),
        padding=[(k // 2, k // 2)], dimension_numbers=('NWC', 'WIO', 'NWC'),
        feature_group_count=c)
    return y + b.astype(x.dtype)


def alibi_slopes(n):
    return 2.0 ** (-8.0 * jnp.arange(1, n + 1, dtype=jnp.float32) / n)


def dilated_window_attention(q, k, v, window, dilation, slopes):
    B, S, H, Dh = q.shape
    r = dilation
    half = window // (2 * r)
    bq = half
    L = S // r
    nb = -(-L // bq)
    Lp = nb * bq

    def to_sub(t):
        return t.reshape(B, L, r, H, Dh).transpose(0, 2, 1, 3, 4)

    qb = jnp.pad(to_sub(q), ((0, 0), (0, 0), (0, Lp - L), (0, 0), (0, 0)))
    qb = qb.reshape(B, r, nb, bq, H, Dh)
    pad_kv = ((0, 0), (0, 0), (bq, Lp - L + bq), (0, 0), (0, 0))

    def kv_blocks(t):
        t = jnp.pad(to_sub(t), pad_kv).reshape(B, r, nb + 2, bq, H, Dh)
        return jnp.concatenate([t[:, :, :-2], t[:, :, 1:-1], t[:, :, 2:]], axis=3)

    kb = kv_blocks(k)
    vb = kv_blocks(v)
    scores = jnp.einsum('bcnqhd,bcnkhd->bchnqk', qb, kb).astype(jnp.float32) * (Dh ** -0.5)
    q_idx = jnp.arange(nb)[:, None] * bq + jnp.arange(bq)[None, :]
    k_idx = (jnp.arange(nb)[:, None] - 1) * bq + jnp.arange(3 * bq)[None, :]
    rel = k_idx[:, None, :] - q_idx[:, :, None]
    valid = (jnp.abs(rel) <= half) & (k_idx[:, None, :] >= 0) & (k_idx[:, None, :] < L)
    dist = (jnp.abs(rel) * r).astype(jnp.float32)
    scores = scores - slopes.astype(jnp.float32)[:, None, None, None] * dist
    scores = jnp.where(valid, scores, MASK_VALUE)
    lse = jax.nn.logsumexp(scores, axis=-1)
    probs = jnp.exp(scores - lse[..., None])
    out = jnp.einsum('bchnqk,bcnkhd->bcnqhd', probs.astype(v.dtype), vb)
    out = out.reshape(B, r, Lp, H, Dh)[:, :, :L].transpose(0, 2, 1, 3, 4).reshape(B, S, H, Dh)
    lse = lse.reshape(B, r, H, Lp)[..., :L].transpose(0, 3, 1, 2).reshape(B, S, H)
    return out, lse


def setup_inputs(seed: int = 0) -> dict:
    key = jax.random.key(seed)
    ks = jax.random.split(key, 17)
    f32 = jnp.float32

    def nrm(k, shape, scale):
        return jax.random.normal(k, shape, f32) * scale

    return {
        "x": jax.random.normal(ks[0], (BATCH, SEQ, D_MODEL), f32),
        "norm_mix_g": 1.0 + nrm(ks[1], (DEPTH, D_MODEL), 0.05),
        "w_in": nrm(ks[2], (DEPTH, D_MODEL, IN_WIDTH), D_MODEL ** -0.5),
        "b_gate": nrm(ks[3], (DEPTH, N_BRANCH * D_MODEL), 0.01),
        "conv_dw_w": nrm(ks[4], (DEPTH, CONV_WIDTH, CONV_CH), CONV_WIDTH ** -0.5),
        "conv_dw_b": nrm(ks[5], (DEPTH, CONV_CH), 0.01),
        "conv_ln_g": 1.0 + nrm(ks[6], (DEPTH, CONV_CH), 0.05),
        "conv_ln_b": nrm(ks[7], (DEPTH, CONV_CH), 0.01),
        "w_conv_out": nrm(ks[8], (DEPTH, CONV_CH, D_MODEL), CONV_CH ** -0.5),
        "w_attn_out": nrm(ks[9], (DEPTH, SLOT_WIDTH, D_MODEL), SLOT_WIDTH ** -0.5),
        "w_out": nrm(ks[10], (DEPTH, D_MODEL, D_MODEL), D_MODEL ** -0.5),
        "norm_ffn_g": 1.0 + nrm(ks[11], (DEPTH, D_MODEL), 0.05),
        "w_up": nrm(ks[12], (DEPTH, D_MODEL, 2 * D_FF), D_MODEL ** -0.5),
        "ffn_dw_w": nrm(ks[13], (DEPTH, FFN_CONV_WIDTH, 2 * D_FF), FFN_CONV_WIDTH ** -0.5),
        "ffn_dw_b": nrm(ks[14], (DEPTH, 2 * D_FF), 0.01),
        "w_down": nrm(ks[15], (DEPTH, D_FF, D_MODEL), D_FF ** -0.5),
        "norm_final_g": 1.0 + nrm(ks[16], (D_MODEL,), 0.05),
    }


def reference(x, norm_mix_g, w_in, b_gate, conv_dw_w, conv_dw_b, conv_ln_g, conv_ln_b,
              w_conv_out, w_attn_out, w_out, norm_ffn_g, w_up, ffn_dw_w, ffn_dw_b,
              w_down, norm_final_g):
    B, S, D = x.shape
    slopes = alibi_slopes(N_ATTN_HEADS)
    split_idx = list(np.cumsum(SPLIT_SIZES))
    h = x
    for l in range(DEPTH):
        u = rms_norm(h, norm_mix_g[l])
        proj = u @ w_in[l]
        conv_a, conv_gate, q, k, v, gate_logits = jnp.split(proj, split_idx, axis=-1)

        z = conv_a * jax.nn.sigmoid(conv_gate)
        z = depthwise_conv(z, conv_dw_w[l], conv_dw_b[l])
        z = jax.nn.silu(layer_norm(z, conv_ln_g[l], conv_ln_b[l]))
        conv_out = z @ w_conv_out[l]

        q = q.reshape(B, S, N_ATTN_HEADS, HEAD_DIM)
        k = k.reshape(B, S, N_ATTN_HEADS, HEAD_DIM)
        v = v.reshape(B, S, N_ATTN_HEADS, HEAD_DIM)
        outs, lses = [], []
        for g, (window, dilation) in enumerate(ATTN_GROUPS):
            hs = slice(g * HEADS_PER_GROUP, (g + 1) * HEADS_PER_GROUP)
            o, lse = dilated_window_attention(q[:, :, hs], k[:, :, hs], v[:, :, hs],
                                              window, dilation, slopes[hs])
            outs.append(o)
            lses.append(lse)
        mix_w = jax.nn.softmax(jnp.stack(lses, axis=0), axis=0)
        attn = jnp.sum(mix_w[..., None] * jnp.stack(outs, axis=0).astype(jnp.float32), axis=0)
        attn_out = attn.astype(x.dtype).reshape(B, S, SLOT_WIDTH) @ w_attn_out[l]

        gates = jax.nn.sigmoid(gate_logits + b_gate[l]).reshape(B, S, N_BRANCH, D)
        merged = gates[:, :, 0] * conv_out + gates[:, :, 1] * attn_out
        h = h + merged @ w_out[l]

        un = rms_norm(h, norm_ffn_g[l])
        up = depthwise_conv(un @ w_up[l], ffn_dw_w[l], ffn_dw_b[l])
        a, val = jnp.split(up, 2, axis=-1)
        h = h + (jax.nn.silu(a) * val) @ w_down[l]
    return rms_norm(h, norm_final_g)
```

```python
from contextlib import ExitStack

import numpy as np
import concourse.bass as bass
import concourse.mybir as mybir
from concourse.bass_utils import run_bass_kernel_spmd

F32 = mybir.dt.float32
BF16 = mybir.dt.bfloat16
I32 = mybir.dt.int32
AF = mybir.ActivationFunctionType
ALU = mybir.AluOpType

S_LEN = 4096
D = 1024
NCORES = 8
GROUPS = ((1, 4096), (4, 1024), (16, 256))
VROW = 24 * 65
V_G1, V_BGA, V_BGB, V_CB, V_LNG, V_LNB, V_G2, V_FB, V_CW, V_FW, V_N = 0, 8, 16, 24, 32, 40, 48, 56, 100, 348, 480


class Buf:
    __slots__ = ("w", "r", "sem", "dcnt")

    def __init__(self):
        self.w = {}
        self.r = {}
        self.sem = None
        self.dcnt = 0


class Sched:
    ENGS = ("pe", "act", "dve", "pool", "sp")

    def __init__(self, nc, stack):
        self.nc = nc
        self.stack = stack
        self.semobjs = {}
        for e in self.ENGS:
            self.semobjs[("e", e)] = stack.enter_context(nc.semaphore("s_" + e))
        self.tick = {e: 0 for e in self.ENGS}
        self.seen = {e: {} for e in self.ENGS}
        self.q = {e: [] for e in self.ENGS}
        self.dma_bufs = []

    def _deps(self, eng, reads, writes, multi, chain=False):
        need = {}

        def add(d):
            for k, v in d.items():
                if need.get(k, 0) < v:
                    need[k] = v

        for b in reads:
            add(b.w)
        for b in writes:
            add(b.r)
            if not multi:
                add(b.w)
        seen = self.seen[eng]
        waits = []
        for k, v in need.items():
            if eng == "pe" and k == ("e", "pe"):
                continue
            if chain and k == ("e", eng):
                continue
            if seen.get(k, 0) >= v:
                continue
            seen[k] = v
            waits.append((k, v))
        return waits

    def _mark(self, key, val, reads, writes, multi):
        for b in reads:
            b.r[key] = val
        for b in writes:
            if multi:
                b.w[key] = val
            else:
                b.w = {key: val}
                b.r = {}

    def op(self, eng, fn, reads=(), writes=(), multi=False, chain=False):
        waits = self._deps(eng, reads, writes, multi, chain)
        self.tick[eng] += 1
        key = ("e", eng)
        self.q[eng].append((waits, fn, key))
        self._mark(key, self.tick[eng], reads, writes, multi)

    def dma(self, eng, fn, ndma, sbuf_buf, reads=(), writes=(), multi=False):
        waits = self._deps(eng, reads, writes, multi)
        key = ("d", id(sbuf_buf))
        if sbuf_buf.sem is None:
            sbuf_buf.sem = self.stack.enter_context(self.nc.semaphore("d%d" % len(self.semobjs)))
            self.semobjs[key] = sbuf_buf.sem
            self.dma_bufs.append(sbuf_buf)
        sbuf_buf.dcnt += 16 * ndma
        self.q[eng].append((waits, fn, key))
        self._mark(key, sbuf_buf.dcnt, reads, writes, multi)

    def drain_dmas(self):
        seen = self.seen["sp"]
        waits = []
        for b in self.dma_bufs:
            k = ("d", id(b))
            if seen.get(k, 0) < b.dcnt:
                seen[k] = b.dcnt
                waits.append((k, b.dcnt))
        self.q["sp"].append((waits, None, None))

    def emit(self):
        nc = self.nc
        semobjs = self.semobjs
        with nc.Block() as blk:
            for ename, deco in (("pe", blk.tensor), ("act", blk.scalar), ("dve", blk.vector),
                                ("pool", blk.gpsimd), ("sp", blk.sync)):
                items = self.q[ename]
                self.q[ename] = []
                if not items:
                    continue

                def body(e, items=items):
                    for waits, fn, key in items:
                        for k, v in waits:
                            e.wait_ge(semobjs[k], v)
                        if fn is None:
                            continue
                        if key[0] == "e":
                            fn(e).then_inc(semobjs[key], 1)
                        else:
                            fn(e, semobjs[key])

                deco(body)


def cap(base, dims):
    return bass.AP(tensor=base.tensor, offset=base.offset, ap=[list(base.ap[0])] + [list(d) for d in dims])


def build(debug=False, nphase=99):
    nc = bass.Bass("TRN2", target_bir_lowering=False)
    ext = "ExternalOutput" if debug else "Internal"

    def din(name, shape):
        return nc.dram_tensor(name, list(shape), F32, kind="ExternalInput")

    x_d = din("x", [S_LEN, D]).ap()
    w_a_d = din("w_a", [8, 128, 8, 128]).ap()
    w_g_d = din("w_g", [8, 128, 8, 128]).ap()
    w_q_d = din("w_q", [12, 128, 8, 128]).ap()
    w_k_d = din("w_k", [12, 128, 8, 128]).ap()
    w_v_d = din("w_v", [128, 8, 1536]).ap()
    w_gA_d = din("w_gA", [128, 8, 1024]).ap()
    w_gB_d = din("w_gB", [128, 8, 1024]).ap()
    w_co_d = din("w_co", [128, 8, 1024]).ap()
    w_ao_d = din("w_ao", [128, 4, 1024]).ap()
    w_o_d = din("w_o", [128, 8, 1024]).ap()
    w_ua_d = din("w_ua", [22, 128, 8, 128]).ap()
    w_uv_d = din("w_uv", [22, 128, 8, 128]).ap()
    w_d_d = din("w_d", [128, 22, 1024]).ap()
    vecs_d = din("vecs", [128, V_N]).ap()
    gF_d = din("gF", [1, D]).ap()
    out_d = nc.dram_tensor("out", [S_LEN, D], F32, kind="ExternalOutput").ap()

    y_scr = nc.dram_tensor("y_scr", [1024, S_LEN], BF16, kind=ext)
    mA_scr = nc.dram_tensor("mA_scr", [1024, S_LEN], BF16, kind=ext)
    v_scr = nc.dram_tensor("v_scr", [S_LEN, VROW], BF16, kind=ext)
    attn_scr = nc.dram_tensor("attn_scr", [512, S_LEN], BF16, kind=ext)
    h_scr = nc.dram_tensor("h_scr", [S_LEN, D], F32, kind=ext)
    act_scr = nc.dram_tensor("act_scr", [2816, S_LEN], BF16, kind=ext)
    if debug:
        u_dbg = nc.dram_tensor("u_dbg", [1024, S_LEN], BF16, kind="ExternalOutput")

    def dap(t, offset, dims):
        return bass.AP(tensor=t, offset=offset, ap=[list(d) for d in dims])

    top = ExitStack()
    with top:
        S = Sched(nc, top)

        _cnt = [0]

        def sb(st, name, shape, dt):
            _cnt[0] += 1
            return st.enter_context(nc.sbuf_tensor("t%d_%s" % (_cnt[0], name), list(shape), dt))

        def ACT(out, in_, func, reads, writes, bias=None, scale=None, accum=None, multi=False, chain=False):
            kw = {}
            if bias is not None:
                kw["bias"] = bias
            if scale is not None:
                kw["scale"] = scale
            if accum is not None:
                kw["accum_out"] = accum
            S.op("act", lambda e: e.activation(out=out, in_=in_, func=func, **kw), reads, writes, multi, chain)

        def TT(out, in0, in1, op, reads, writes, eng="dve", multi=False, chain=False):
            S.op(eng, lambda e: e.tensor_tensor(out=out, in0=in0, in1=in1, op=op), reads, writes, multi, chain)

        def TS(out, in0, s1, op0, reads, writes, s2=None, op1=None, eng="dve", multi=False):
            if op1 is None:
                S.op(eng, lambda e: e.tensor_scalar(out=out, in0=in0, scalar1=s1, scalar2=None, op0=op0),
                     reads, writes, multi)
            else:
                S.op(eng, lambda e: e.tensor_scalar(out=out, in0=in0, scalar1=s1, scalar2=s2, op0=op0, op1=op1),
                     reads, writes, multi)

        def STT(out, in0, scalar, in1, op0, op1, reads, writes, multi=False, chain=False):
            S.op("dve", lambda e: e.scalar_tensor_tensor(out=out, in0=in0, scalar=scalar, in1=in1, op0=op0, op1=op1),
                 reads, writes, multi, chain)

        def COPY(out, in_, reads, writes, eng="dve", multi=False, chain=False):
            S.op(eng, lambda e: e.tensor_copy(out=out, in_=in_), reads, writes, multi, chain)

        def RECIP(out, in_, reads, writes):
            S.op("dve", lambda e: e.reciprocal(out=out, in_=in_), reads, writes)

        def MEMSET(out, val, writes, eng="dve"):
            S.op(eng, lambda e: e.memset(out, val), (), writes)

        def MM(out, pairs, reads, writes, start=True, stop=True, multi=False):
            def fn(e):
                n = len(pairs)
                ins = None
                for i, (l, r) in enumerate(pairs):
                    ins = e.matmul(out, lhsT=l, rhs=r, start=(start and i == 0), stop=(stop and i == n - 1))
                return ins
            S.op("pe", fn, reads, writes, multi)

        def TRS(outs_ins, ident, reads, writes):
            def fn(e):
                ins = None
                for o, i in outs_ins:
                    ins = e.transpose(o, i, ident)
                return ins
            S.op("pe", fn, reads, writes)

        def DMA(eng, pairs, sbuf_buf, reads, writes, multi=False):
            def fn(e, sem):
                for o, i in pairs:
                    e.dma_start(out=o, in_=i).then_inc(sem, 16)
            S.dma(eng, fn, len(pairs), sbuf_buf, reads, writes, multi)

        ps = top.enter_context(nc.psum_tensor("ps", [128, 4096], F32))
        psb = ps.bitcast(BF16)
        PB = [Buf() for _ in range(8)]
        PH = [[Buf(), Buf()] for _ in range(8)]

        def bank(k):
            return ps[:, k * 512:(k + 1) * 512]

        uT = sb(top, "uT", [128, 8, S_LEN], BF16)
        uTB = [Buf() for _ in range(8)]
        vecs = sb(top, "vecs", [128, V_N], F32)
        vecsB = Buf()
        ident = sb(top, "ident", [128, 128], BF16)
        identB = Buf()
        eps6 = sb(top, "eps6", [128, 1], F32)
        eps5 = sb(top, "eps5", [128, 1], F32)
        neghalf = sb(top, "neghalf", [128, 1], F32)
        constB = Buf()
        iot = sb(top, "iot", [128, 448], I32)
        iotB = Buf()
        DW = sb(top, "DW", [128, 448], F32)
        VW = sb(top, "VW", [128, 448], F32)
        distB = Buf()

        def vcol(c, n=1):
            return vecs[:, c:c + n]

        DMA("sp", [(vecs[:], vecs_d)], vecsB, (), [vecsB])
        MEMSET(eps6[:], 1e-6, [constB])
        S.op("dve", lambda e: e.memset(eps5[:], 1e-5), (), [constB], multi=True)
        S.op("dve", lambda e: e.memset(neghalf[:], -0.5), (), [constB], multi=True)
        S.op("pool", lambda e: e.iota(iot[:, 0:128], [[-1, 128]], base=0, channel_multiplier=1), (), [iotB])
        COPY(DW[:, 0:128], iot[:, 0:128], [iotB], [distB])
        TS(ident[:], DW[:, 0:128], 0.0, ALU.is_equal, [distB], [identB])
        S.op("pool", lambda e: e.iota(iot[:], [[-1, 448]], base=192, channel_multiplier=1), [distB], [iotB])
        COPY(DW[:], iot[:], [iotB, identB], [distB])
        TS(VW[:], DW[:], -1.0, ALU.mult, [distB], [constB], multi=True)
        TT(DW[:], DW[:], VW[:], ALU.max, [distB, constB], [distB])
        TS(VW[:], DW[:], 64.0, ALU.is_le, [distB], [constB], multi=True)

        def rms_stats(src, srcB, i, tmp, pool_pow=False, pool_scale=False):
            junk, junkB, ssq, ssqB, std, stdB, rstd, rstdB, xs, xsB, TBs = tmp
            j = i % len(xs)
            ACT(junk[:], src[:], AF.Square, [srcB], [junkB, ssqB[j]], accum=ssq[j][:])
            if pool_pow:
                TS(std[j][:], ssq[j][:], 1.0 / D, ALU.mult, [ssqB[j]], [stdB[j]], s2=1e-6, op1=ALU.add, eng="pool")
                TT(rstd[j][:], std[j][:], neghalf[:], ALU.pow, [stdB[j], constB], [rstdB[j]], eng="pool")
            else:
                ACT(std[j][:], ssq[j][:], AF.Sqrt, [ssqB[j], constB], [stdB[j]], bias=eps6[:], scale=1.0 / D)
                RECIP(rstd[j][:], std[j][:], [stdB[j]], [rstdB[j]])
            if pool_scale:
                TS(xs[j][:], src[:], rstd[j][:], ALU.mult, [srcB, rstdB[j]], [xsB[j]], s2=1.0, op1=ALU.mult,
                   eng="pool")
            else:
                TS(xs[j][:], src[:], rstd[j][:], ALU.mult, [srcB, rstdB[j]], [xsB[j]])

        def rms_tr(i, tmp):
            xs, xsB, TBs = tmp[-3], tmp[-2], tmp[-1]
            j = i % len(xs)
            TB = TBs[i % len(TBs)]
            TRS([(psb[:, TB * 1024 + kc * 128: TB * 1024 + (kc + 1) * 128], xs[j][:, kc * 128:(kc + 1) * 128])
                 for kc in range(8)], ident[:], [xsB[j], identB], [PB[TB]])

        def rms_back(i, gcol, tmp):
            TB = tmp[-1][i % len(tmp[-1])]
            TT(uT[:, :, i * 128:(i + 1) * 128],
               cap(psb[:, TB * 1024: TB * 1024 + 1], [[128, 8], [1, 128]]),
               cap(vcol(gcol), [[1, 8], [0, 128]]), ALU.mult,
               [PB[TB], vecsB], [uTB[i // 4]], multi=True)

        def rms_tmp(st, TBs, nbuf=2):
            junk = sb(st, "junk", [128, D], BF16)
            ssq = [sb(st, "ssq%d" % i, [128, 1], F32) for i in range(nbuf)]
            std = [sb(st, "std%d" % i, [128, 1], F32) for i in range(nbuf)]
            rstd = [sb(st, "rstd%d" % i, [128, 1], F32) for i in range(nbuf)]
            xs = [sb(st, "xs%d" % i, [128, D], BF16) for i in range(nbuf)]
            mk = lambda: [Buf() for _ in range(nbuf)]
            return (junk, Buf(), ssq, mk(), std, mk(), rstd, mk(), xs, mk(), TBs)

        with ExitStack() as st:
            NX = 8
            xt = [sb(st, "xt%d" % i, [128, D], F32) for i in range(NX)]
            xtB = [Buf() for _ in range(NX)]
            tmp = rms_tmp(st, (0, 1, 2, 3, 4, 5), nbuf=6)
            for i in range(32 + 6):
                if i < 32:
                    DMA("sp", [(xt[i % NX][:], x_d[i * 128:(i + 1) * 128, :])], xtB[i % NX], (), [xtB[i % NX]])
                    rms_stats(xt[i % NX], xtB[i % NX], i, tmp, pool_scale=(i % 3 != 0))
                if 3 <= i < 35:
                    rms_tr(i - 3, tmp)
                if i >= 6:
                    rms_back(i - 6, V_G1, tmp)
            if debug:
                DMA("sp", [(dap(u_dbg, 0, [[S_LEN, 128], [128 * S_LEN, 8], [1, S_LEN]]), uT[:])], uTB[0], uTB, [])
            S.drain_dmas()
            S.emit()
        if nphase <= 0:
            return nc

        yscrB = [Buf() for _ in range(8)]
        stW1 = ExitStack()
        wco = sb(stW1, "wco", [128, 8, 1024], BF16)
        wgA = sb(stW1, "wgA", [128, 8, 1024], BF16)
        wcoB, wgAB = Buf(), Buf()
        wv = sb(stW1, "wv", [128, 8, 1536], BF16)
        wvB = Buf()
        vscrB = Buf()
        with ExitStack() as st:
            z = [sb(st, "z%d" % i, [128, S_LEN + 30], BF16) for i in range(2)]
            zB = [[Buf() for _ in range(8)] for _ in range(2)]
            wa = [sb(st, "wa%d" % i, [128, 8, 128], BF16) for i in range(2)]
            wg = [sb(st, "wg%d" % i, [128, 8, 128], BF16) for i in range(2)]
            waB = [Buf(), Buf()]
            wgB_ = [Buf(), Buf()]
            dg = [sb(st, "dg%d" % i, [128, 31, 128], BF16) for i in range(2)]
            dgB = [Buf(), Buf()]
            sg = [sb(st, "sg%d" % i, [128, 512], F32) for i in range(2)]
            sgB = [Buf(), Buf()]
            yb = [sb(st, "yb%d" % i, [128, S_LEN], BF16) for i in range(2)]
            ybB = [Buf(), Buf()]
            for b in range(2):
                MEMSET(z[b][:], 0.0, zB[b])
            A_, G_, Y_ = (0, 1), (2, 3), (4, 5, 6)
            NPE_TAPS = 23
            def build_dg(cc):
                bb = cc % 2
                TT(dg[bb][:], cap(ident[:, 0:1], [[0, 31], [1, 128]]),
                   cap(vcol(V_CW + cc * 31), [[1, 31], [0, 128]]), ALU.mult, [identB, vecsB], [dgB[bb]])

            for c in range(8):
                b = c % 2
                DMA("pool", [(wa[b][:], w_a_d[c])], waB[b], (), [waB[b]])
                DMA("pool", [(wg[b][:], w_g_d[c])], wgB_[b], (), [wgB_[b]])
                if c == 2:
                    DMA("pool", [(wco[:, 0:4], w_co_d[:, 0:4]), (wco[:, 4:8], w_co_d[:, 4:8])], wcoB, (), [wcoB])
                    DMA("pool", [(wgA[:, 0:4], w_gA_d[:, 0:4]), (wgA[:, 4:8], w_gA_d[:, 4:8])], wgAB, (), [wgAB])
                if c == 4:
                    DMA("pool", [(wv[:, 2 * q:2 * q + 2], w_v_d[:, 2 * q:2 * q + 2]) for q in range(4)], wvB, (), [wvB])
                if c == 0:
                    build_dg(0)
                for step in range(10):
                    tt = step
                    if step == 3 and c + 1 < 8:
                        build_dg(c + 1)
                    if tt < 8:
                        tsl = slice(tt * 512, (tt + 1) * 512)
                        MM(bank(A_[tt % 2]), [(wa[b][:, kc, :], uT[:, kc, tsl]) for kc in range(8)],
                           [waB[b], uTB[tt]], [PB[A_[tt % 2]]])
                        MM(bank(G_[tt % 2]), [(wg[b][:, kc, :], uT[:, kc, tsl]) for kc in range(8)],
                           [wgB_[b], uTB[tt]], [PB[G_[tt % 2]]])
                        ACT(sg[tt % 2][:], bank(G_[tt % 2]), AF.Sigmoid, [PB[G_[tt % 2]]], [sgB[tt % 2]])
                        TT(z[b][:, 15 + tt * 512: 15 + (tt + 1) * 512], bank(A_[tt % 2]), sg[tt % 2][:], ALU.mult,
                           [PB[A_[tt % 2]], sgB[tt % 2]], [zB[b][tt]])
                    ct = step - 2
                    if 0 <= ct < 8:
                        rd = [dgB[b]] + [zB[b][t] for t in (ct - 1, ct, ct + 1) if 0 <= t < 8]
                        yk = Y_[ct % 3]
                        npe = 31 if (c == 7 and ct >= 6) else NPE_TAPS
                        MM(bank(yk),
                           [(dg[b][:, k, :], z[b][:, ct * 512 + k: ct * 512 + k + 512]) for k in range(npe)],
                           rd, [PB[yk]])
                        for k in range(npe, 31):
                            STT(bank(yk), z[b][:, ct * 512 + k: ct * 512 + k + 512], vcol(V_CW + c * 31 + k),
                                bank(yk), ALU.mult, ALU.add, rd[1:] + [PB[yk], vecsB], [PB[yk]],
                                chain=(k > npe))
                        ACT(yb[b][:, ct * 512:(ct + 1) * 512], bank(yk), AF.Identity,
                            [PB[yk], vecsB], [ybB[b]], bias=vcol(V_CB + c), multi=True)
                DMA("sp", [(y_scr.ap()[c * 128:(c + 1) * 128, :], yb[b][:])], ybB[b], [ybB[b]], [yscrB[c]])
            S.drain_dmas()
            S.emit()
        if nphase <= 1:
            stW1.close()
            return nc

        mAscrB = [Buf() for _ in range(8)]
        with ExitStack() as st:
            onesb = sb(st, "onesb", [128, 128], BF16)
            onesB = Buf()
            MEMSET(onesb[:], 1.0 / 1024.0, [onesB])
            yt = [sb(st, "yt%d" % i, [128, 8, 512], BF16) for i in range(2)]
            ytB = [Buf(), Buf()]
            mean1 = sb(st, "mean", [128, 512], F32)
            var = sb(st, "var", [128, 512], F32)
            rstd1 = sb(st, "rstdl", [128, 512], F32)
            mean, rstd = [mean1, mean1], [rstd1, rstd1]
            msq = var
            mB_, rB_, vB_ = Buf(), Buf(), Buf()
            meanB, msqB, varB, rstdB = [mB_, mB_], vB_, vB_, [rB_, rB_]
            vt = [sb(st, "vt%d" % i, [128, 24, 65], BF16) for i in range(2)]
            vtB = [Buf(), Buf()]
            for b in range(2):
                MEMSET(vt[b][:], 1.0, [vtB[b]])
            VB_ = (6, 7)
            vctr = [0]

            def vproj(tt):
                for i in range(4 * tt, 4 * tt + 4):
                    b = i % 2
                    for n in range(3):
                        bk = VB_[vctr[0] % 2]
                        vctr[0] += 1
                        MM(bank(bk), [(uT[:, kc, i * 128:(i + 1) * 128], wv[:, kc, n * 512:(n + 1) * 512])
                                      for kc in range(8)], [uTB[i // 4], wvB], [PB[bk]])
                        o_ap = vt[b][:, n * 8:(n + 1) * 8, 0:64]
                        i_ap = cap(ps[:, bk * 512: bk * 512 + 1], [[64, 8], [1, 64]])
                        COPY(o_ap, i_ap, [PB[bk]], [vtB[b]], multi=True)
                    DMA("sp", [(dap(v_scr, i * 128 * VROW, [[VROW, 128], [65, 24], [1, 65]]), vt[b][:])],
                        vtB[b], [vtB[b]], [vscrB], multi=True)
            tnF = sb(st, "tnF", [128, 8, 512], F32)
            tnFB = Buf()
            s_ = sb(st, "s_", [128, 8, 512], BF16)
            sB = [Buf() for _ in range(8)]
            ga = [sb(st, "ga%d" % i, [128, 512], F32) for i in range(2)]
            gaB = [Buf(), Buf()]
            mA1 = sb(st, "mA", [128, 8, 512], BF16)
            mA = [mA1, mA1]
            mAB1 = Buf()
            mAB = [mAB1, mAB1]
            ysq = sb(st, "ysq", [128, 8, 512], BF16)
            ysqB = Buf()
            s2 = sb(st, "s_2", [128, 8, 512], BF16)
            sT = [s_, s2]
            sBB = [sB, [Buf() for _ in range(8)]]
            M_, Q_, C_, GA_ = 0, 1, (2, 3), (4, 5)

            def load1b(tt):
                b = tt % 2
                DMA("sp", [(yt[b][:], dap(y_scr, tt * 512, [[S_LEN, 128], [128 * S_LEN, 8], [1, 512]]))],
                    ytB[b], yscrB, [ytB[b]])

            def stats1b(tt):
                b = tt % 2
                TT(ysq[:], yt[b][:], yt[b][:], ALU.mult, [ytB[b]], [ysqB], eng=("dve" if tt == 0 else "pool"))
                MM(bank(M_), [(onesb[:], yt[b][:, c, :]) for c in range(8)], [onesB, ytB[b]], [PB[M_]])
                MM(bank(Q_), [(onesb[:], ysq[:, c, :]) for c in range(8)], [onesB, ysqB], [PB[Q_]])
                COPY(mean[b][:], bank(M_), [PB[M_]], [meanB[b]])
                TT(msq[:], mean[b][:], mean[b][:], ALU.mult, [meanB[b]], [msqB])
                TT(var[:], bank(Q_), msq[:], ALU.subtract, [PB[Q_], msqB], [varB])
                ACT(var[:], var[:], AF.Ln, [varB, constB], [varB], bias=eps5[:], scale=1.0)
                ACT(rstd[b][:], var[:], AF.Exp, [varB], [rstdB[b]], scale=-0.5, chain=True)
                TT(tnF[:], yt[b][:], cap(mean[b][:, 0:1], [[0, 8], [1, 512]]), ALU.subtract,
                   [ytB[b], meanB[b]], [tnFB], eng="pool")
                TT(tnF[:], tnF[:], cap(rstd[b][:, 0:1], [[0, 8], [1, 512]]), ALU.mult,
                   [tnFB, rstdB[b]], [tnFB], eng="pool", chain=True)

            def silu1b(tt):
                b = tt % 2
                for c in range(8):
                    ACT(sT[b][:, c, :], tnF[:, c, :], AF.Silu, [tnFB, vecsB], [sBB[b][c]],
                        bias=vcol(V_LNB + c), scale=vcol(V_LNG + c))

            def back1b(tt, dc):
                b = tt % 2
                tsl = slice(tt * 512, (tt + 1) * 512)
                k = dc % 2
                dsl = slice(dc * 128, (dc + 1) * 128)
                MM(bank(C_[k]), [(wco[:, kc, dsl], sT[b][:, kc, :]) for kc in range(8)], [wcoB] + sBB[b],
                   [PB[C_[k]]])
                MM(bank(GA_[k]), [(wgA[:, kc, dsl], uT[:, kc, tsl]) for kc in range(8)],
                   [wgAB, uTB[tt]], [PB[GA_[k]]])
                ACT(ga[k][:], bank(GA_[k]), AF.Sigmoid, [PB[GA_[k]], vecsB], [gaB[k]], bias=vcol(V_BGA + dc))
                TT(mA[b][:, dc, :], bank(C_[k]), ga[k][:], ALU.mult, [PB[C_[k]], gaB[k]], [mAB[b]], multi=True)

            def store1b(tt):
                b = tt % 2
                DMA("sp", [(dap(mA_scr, tt * 512, [[S_LEN, 128], [128 * S_LEN, 8], [1, 512]]), mA[b][:])],
                    mAB[b], [mAB[b]], [mAscrB[tt]])

            load1b(0)
            load1b(1)
            for tt in range(9):
                if tt < 8:
                    stats1b(tt)
                    vproj(tt)
                if tt >= 1:
                    for dc in range(8):
                        back1b(tt - 1, dc)
                    store1b(tt - 1)
                if tt < 8:
                    silu1b(tt)
                if tt + 2 < 8:
                    load1b(tt + 2)
            S.drain_dmas()
            S.emit()
        stW1.close()
        if nphase <= 2:
            return nc

        attnscrB = [Buf() for _ in range(4)]
        with ExitStack() as st:
            qT = [sb(st, "qT%d" % i, [128, S_LEN], BF16) for i in range(2)]
            kT = [sb(st, "kT%d" % i, [128, S_LEN], BF16) for i in range(2)]
            qTB, kTB = [Buf(), Buf()], [Buf(), Buf()]
            vb = [sb(st, "vb%d" % i, [128, 36, 2, 128], BF16) for i in range(2)]
            vbB = [Buf(), Buf()]
            wq = [sb(st, "wq%d" % i, [128, 8, 128], BF16) for i in range(2)]
            wk = [sb(st, "wk%d" % i, [128, 8, 128], BF16) for i in range(2)]
            wqB, wkB = [Buf(), Buf()], [Buf(), Buf()]
            E2 = [sb(st, "E2%d" % i, [128, 2, 256], BF16) for i in range(2)]
            E2B = [Buf(), Buf()]
            Ef = sb(st, "Ef", [128, 256], F32)
            EfB = Buf()
            E0 = [sb(st, "E0%d" % i, [128, 2, 128], BF16) for i in range(2)]
            E2b = [sb(st, "E2b%d" % i, [128, 2, 256], BF16) for i in range(2)]
            Ef0 = sb(st, "Ef0", [128, 128], F32)
            Ef0B = Buf()
            osum = sb(st, "osum", [128, 2, S_LEN], F32)
            osumB = [Buf(), Buf()]
            at = sb(st, "at", [128, S_LEN], BF16)
            atB = Buf()
            NSLOT = 3
            LAG = 2
            pT = [sb(st, "pT%d" % i, [128, 512], BF16) for i in range(NSLOT)]
            pTB = [Buf() for _ in range(NSLOT)]
            rl = sb(st, "rl", [128, 512], F32)
            rlB = Buf()
            rq = [sb(st, "rq%d" % i, [64, 512], F32) for i in range(2)]
            rqB = [Buf(), Buf()]
            QK_ = (0, 1)
            SBK = ((0, 1), (2, 3), (4, 5))
            OB_ = (6, 7)
            seq = [(p, g) for p in range(3) for g in range(3)] + [(3, 2), (3, 1), (3, 0)]

            def proj(n, hook=None):
                p, g = seq[n]
                r, L = GROUPS[g]
                b = n % 2
                nb = L // 128
                DMA("pool", [(wq[b][:], w_q_d[4 * g + p])], wqB[b], (), [wqB[b]])
                DMA("pool", [(wk[b][:], w_k_d[4 * g + p])], wkB[b], (), [wkB[b]])
                if n < 2:
                    MEMSET(vb[b][:], 1.0, [vbB[b]], eng="pool")
                for h in range(2):
                    head = 8 * g + 2 * p + h
                    slope = 2.0 ** (-(head + 1) / 3.0)
                    if g == 2:
                        views = ((E2[b], 192), (E2b[b], 64))
                    else:
                        views = ((E2[b], 128),)
                    for (Et, c0) in views:
                        ACT(Ef[:], DW[:, c0:c0 + 256], AF.Exp, [distB], [EfB], scale=-slope * r)
                        TT(Et[:, h, :], Ef[:], VW[:, c0:c0 + 256], ALU.mult, [EfB, constB], [E2B[b]], multi=True)
                    if g != 2:
                        ACT(Ef0[:], DW[:, 192:320], AF.Exp, [distB], [Ef0B], scale=-slope * r)
                        TT(E0[b][:, h, :], Ef0[:], VW[:, 192:320], ALU.mult, [Ef0B, constB], [E2B[b]], multi=True)
                pairs = []
                hc = (8 * g + 2 * p) * 65
                for h in range(2):
                    hh = hc + 65 * h
                    if g == 2:
                        for c in range(r):
                            pairs.append((vb[b][:, 2 * c: 2 * c + 2, h, 0:64],
                                          dap(v_scr, c * VROW + hh, [[r * VROW, 128], [128 * r * VROW, 2], [1, 64]])))
                    else:
                        for c in range(r):
                            base = c * (nb + 1)
                            if nb > 1:
                                pairs.append((vb[b][:, base + 1: base + nb, h, 0:64],
                                              dap(v_scr, (64 * r + c) * VROW + hh,
                                                  [[r * VROW, 128], [128 * r * VROW, nb - 1], [1, 64]])))
                            pairs.append((vb[b][0:64, base, h, 0:64],
                                          dap(v_scr, c * VROW + hh, [[r * VROW, 64], [1, 64]])))
                            pairs.append((vb[b][0:64, base + nb, h, 0:64],
                                          dap(v_scr, ((L - 64) * r + c) * VROW + hh, [[r * VROW, 64], [1, 64]])))
                DMA("sp", pairs, vbB[b], [vscrB], [vbB[b]])
                ni = 512 // r
                for tt in range(8):
                    tsl = slice(tt * 512, (tt + 1) * 512)
                    MM(bank(QK_[0]), [(wq[b][:, kc, :], uT[:, kc, tsl]) for kc in range(8)],
                       [wqB[b], uTB[tt]], [PB[QK_[0]]])
                    MM(bank(QK_[1]), [(wk[b][:, kc, :], uT[:, kc, tsl]) for kc in range(8)],
                       [wkB[b], uTB[tt]], [PB[QK_[1]]])
                    if r == 1:
                        qo, ko = qT[b][:, tsl], kT[b][:, tsl]
                        qi, ki = bank(QK_[0]), bank(QK_[1])
                    else:
                        qo = cap(qT[b][:, tt * ni: tt * ni + 1], [[L, r], [1, ni]])
                        ko = cap(kT[b][:, tt * ni: tt * ni + 1], [[L, r], [1, ni]])
                        qi = cap(ps[:, QK_[0] * 512: QK_[0] * 512 + 1], [[1, r], [r, ni]])
                        ki = cap(ps[:, QK_[1] * 512: QK_[1] * 512 + 1], [[1, r], [r, ni]])
                    if hook is not None:
                        TS(qo, qi, 0.125, ALU.mult, [PB[QK_[0]]], [qTB[b]], multi=True)
                    else:
                        ACT(qo, qi, AF.Identity, [PB[QK_[0]]], [qTB[b]], scale=0.125, multi=True)
                    COPY(ko, ki, [PB[QK_[1]]], [kTB[b]], multi=True)
                    if hook is not None:
                        hook(tt)

            blk_ctr = [0]

            def attn(n):
                p, g = seq[n]
                first = (n % 3 == 0)
                inline_norm = (n == len(seq) - 1)
                r, L = GROUPS[g]
                b = n % 2
                nb = L // 128
                blocks = []
                for c in range(r):
                    for kb in range(nb + 1):
                        blocks.append((c, kb, blk_ctr[0] % NSLOT))
                        blk_ctr[0] += 1

                def geom(c, kb):
                    if kb == 0:
                        return 64, c * L, c * L, 128
                    if kb == nb:
                        return 64, c * L + L - 64, c * L + L - 128, 128
                    return 128, c * L + 128 * kb - 64, c * L + 128 * (kb - 1), 256

                def front(c, kb, sl):
                    KP, koff, qoff, N = geom(c, kb)
                    bks = SBK[sl]
                    for h in range(2):
                        hp = slice(64 * h, 64 * h + 64)
                        MM(ps[0:KP, bks[h] * 512: bks[h] * 512 + N],
                           [(kT[b][hp, koff:koff + KP], qT[b][hp, qoff:qoff + N])],
                           [kTB[b], qTB[b]], [PB[bks[h]]])
                    src = cap(ps[0:KP, bks[0] * 512: bks[0] * 512 + 1], [[512, 2], [1, N]])
                    ptv = cap(pT[sl][0:KP, 0:1], [[N, 2], [1, N]])
                    ACT(ptv, src, AF.Exp, [PB[bks[0]], PB[bks[1]]], [pTB[sl]])
                    ee = E0[b][0:64, :, 0:128] if kb == 0 else E2[b][0:KP, :, 0:N]
                    TT(ptv, ptv, ee, ALU.mult, [pTB[sl], E2B[b]], [pTB[sl]])

                def pv(out, l, r_, reads, bkb, start):
                    S.op("pe", lambda e: e.matmul(out, lhsT=l, rhs=r_, start=start, stop=True, skip_group_check=True),
                         reads, [PB[bkb]], True)

                pend_norm = []

                def back(c, kb, sl):
                    KP, koff, qoff, N = geom(c, kb)
                    base = c * (nb + 1)
                    if pend_norm and kb % 2 == 1:
                        norm_head(p, pend_norm.pop(), 1)
                    for h in range(2):
                        vl = vb[b][0:KP, base + kb, h, :]
                        if kb >= 1:
                            nq = kb - 1
                            bk = OB_[(nq // 2) % 2]
                            col = bk * 512 + h * 256 + (nq % 2) * 128
                            pv(ps[0:128, col:col + 128], vl, pT[sl][0:KP, h * N: h * N + 128],
                               [vbB[b], pTB[sl]], bk, False)
                        if kb <= nb - 1:
                            nq = kb
                            bk = OB_[(nq // 2) % 2]
                            col = bk * 512 + h * 256 + (nq % 2) * 128
                            pv(ps[0:128, col:col + 128], vl, pT[sl][0:KP, h * N + N - 128: h * N + N],
                               [vbB[b], pTB[sl]], bk, (nq % 2 == 0 and h == 0))
                    if kb >= 2 and kb % 2 == 0:
                        G2 = kb // 2 - 1
                        bk = OB_[G2 % 2]
                        src = cap(ps[0:128, bk * 512: bk * 512 + 1], [[256, 2], [1, 256]])
                        dst = cap(osum[0:128, 0, 256 * G2 * r + c: 256 * G2 * r + c + 1], [[S_LEN, 2], [r, 256]])
                        if first:
                            COPY(dst, src, [PB[bk]], [osumB[0]], multi=True, chain=True)
                        else:
                            TT(dst, dst, src, ALU.add, [PB[bk], osumB[0]], [osumB[0]], multi=True, chain=True)
                        if inline_norm and G2 % 2 == 1:
                            norm_head(p, G2 // 2, 0)
                            pend_norm.append(G2 // 2)

                def front2(c, j, sl):
                    bks = SBK[sl]
                    for h in range(2):
                        hp = slice(64 * h, 64 * h + 64)
                        MM(ps[:, bks[h] * 512: bks[h] * 512 + 256],
                           [(kT[b][hp, c * L + 128 * j: c * L + 128 * j + 128], qT[b][hp, c * L: c * L + 256])],
                           [kTB[b], qTB[b]], [PB[bks[h]]])
                    src = cap(ps[:, bks[0] * 512: bks[0] * 512 + 1], [[512, 2], [1, 256]])
                    ACT(pT[sl][:], src, AF.Exp, [PB[bks[0]], PB[bks[1]]], [pTB[sl]])
                    Et = E2[b] if j == 0 else E2b[b]
                    TT(pT[sl][:], pT[sl][:], Et[:], ALU.mult, [pTB[sl], E2B[b]], [pTB[sl]])

                def back2(c, j, sl):
                    bk = OB_[c % 2]
                    for h in range(2):
                        pv(ps[0:128, bk * 512 + h * 256: bk * 512 + h * 256 + 256],
                           vb[b][:, 2 * c + j, h, :], pT[sl][:, h * 256:(h + 1) * 256],
                           [vbB[b], pTB[sl]], bk, (j == 0 and h == 0))
                    if j == 1:
                        src = ps[0:128, bk * 512:(bk + 1) * 512]
                        dst = cap(osum[0:128, 0, c:c + 1], [[S_LEN, 2], [r, 256]])
                        if first:
                            COPY(dst, src, [PB[bk]], [osumB[0]], multi=True, chain=True)
                        else:
                            TT(dst, dst, src, ALU.add, [PB[bk], osumB[0]], [osumB[0]], multi=True, chain=True)

                if g == 2:
                    blocks = []
                    for c in range(r):
                        for j in range(2):
                            blocks.append((c, j, blk_ctr[0] % NSLOT))
                            blk_ctr[0] += 1
                    ff, bb = front2, back2
                else:
                    ff, bb = front, back
                nblk = len(blocks)
                for i in range(nblk + LAG):
                    if i < nblk:
                        ff(*blocks[i])
                    if i - LAG >= 0:
                        bb(*blocks[i - LAG])
                while pend_norm:
                    norm_head(p, pend_norm.pop(), 1)

            def norm_head(p, tt, h):
                tsl = slice(tt * 512, (tt + 1) * 512)
                ACT(rl[64:128, :], osum[64:128, h, tsl], AF.Ln, [osumB[0]], [rlB])
                ACT(rq[h][0:64, :], rl[64:128, :], AF.Exp, [rlB], [rqB[h]], scale=-1.0, chain=True)
                TT(at[64 * h:64 * h + 64, tsl], osum[0:64, h, tsl], rq[h][0:64, :], ALU.mult,
                   [osumB[0], rqB[h]], [atB], multi=True)

            def norm_tile(p, tt):
                tsl = slice(tt * 512, (tt + 1) * 512)
                for h in range(2):
                    ACT(rl[64:128, :], osum[64:128, h, tsl], AF.Ln, [osumB[0]], [rlB])
                    ACT(rq[h][0:64, :], rl[64:128, :], AF.Exp, [rlB], [rqB[h]], scale=-1.0, chain=True)
                    TT(at[64 * h:64 * h + 64, tsl], osum[0:64, h, tsl], rq[h][0:64, :], ALU.mult,
                       [osumB[0], rqB[h]], [atB], multi=True)

            def norm_store(p):
                DMA("sp", [(attn_scr.ap()[p * 128:(p + 1) * 128, :], at[:])], atB, [atB], [attnscrB[p]])

            pend = None
            proj(0)
            for n in range(len(seq)):
                if n + 1 < len(seq):
                    if pend is not None:
                        pp = pend
                        proj(n + 1, hook=lambda tt: norm_tile(pp, tt))
                        norm_store(pp)
                        pend = None
                    else:
                        proj(n + 1)
                attn(n)
                if n % 3 == 2 and n != len(seq) - 1:
                    pend = seq[n][0]
            norm_store(3)
            S.drain_dmas()
            S.emit()
        if nphase <= 4:
            return nc

        hscrB = Buf()
        with ExitStack() as st:
            wao = sb(st, "wao", [128, 4, 1024], BF16)
            wgBt = sb(st, "wgBt", [128, 8, 1024], BF16)
            wo = sb(st, "wo", [128, 8, 1024], BF16)
            waoB, wgBB, woB = Buf(), Buf(), Buf()
            DMA("pool", [(wgBt[:, 0:4], w_gB_d[:, 0:4]), (wgBt[:, 4:8], w_gB_d[:, 4:8])], wgBB, (), [wgBB])
            DMA("pool", [(wao[:, 0:2], w_ao_d[:, 0:2]), (wao[:, 2:4], w_ao_d[:, 2:4])], waoB, (), [waoB])
            DMA("pool", [(wo[:, 0:4], w_o_d[:, 0:4]), (wo[:, 4:8], w_o_d[:, 4:8])], woB, (), [woB])
            mAt = [sb(st, "mAt%d" % i, [128, 8, 512], BF16) for i in range(2)]
            att = [sb(st, "att%d" % i, [128, 4, 512], BF16) for i in range(2)]
            mAtB, attB = [Buf(), Buf()], [Buf(), Buf()]
            mg = [sb(st, "mg%d" % i, [128, 8, 512], BF16) for i in range(2)]
            mgB = [Buf(), Buf()]
            gb = [sb(st, "gb%d" % i, [128, 512], F32) for i in range(2)]
            gbB = [Buf(), Buf()]
            tm = [sb(st, "tm%d" % i, [128, 512], F32) for i in range(2)]
            tmB = [Buf(), Buf()]
            xt = [sb(st, "xt3%d" % i, [128, D], F32) for i in range(2)]
            xtB = [Buf(), Buf()]
            hT = [sb(st, "hT%d" % i, [128, D], F32) for i in range(2)]
            hTB = [Buf(), Buf()]
            tmp = rms_tmp(st, (6, 7))
            AO_, GB_, H_ = (0, 1), (2, 3), 4

            def loadA(tt):
                b = tt % 2
                DMA("sp", [(mAt[b][:], dap(mA_scr, tt * 512, [[S_LEN, 128], [128 * S_LEN, 8], [1, 512]]))],
                    mAtB[b], [mAscrB[tt]], [mAtB[b]])
                DMA("sp", [(att[b][:], dap(attn_scr, tt * 512, [[S_LEN, 128], [128 * S_LEN, 4], [1, 512]]))],
                    attB[b], attnscrB, [attB[b]])

            def stageA(tt, dc):
                b = tt % 2
                tsl = slice(tt * 512, (tt + 1) * 512)
                k = dc % 2
                dsl = slice(dc * 128, (dc + 1) * 128)
                MM(bank(GB_[k]), [(wgBt[:, kc, dsl], uT[:, kc, tsl]) for kc in range(8)],
                   [wgBB, uTB[tt]], [PB[GB_[k]]])
                MM(bank(AO_[k]), [(wao[:, kc, dsl], att[b][:, kc, :]) for kc in range(4)],
                   [waoB, attB[b]], [PB[AO_[k]]])
                ACT(gb[k][:], bank(GB_[k]), AF.Sigmoid, [PB[GB_[k]], vecsB], [gbB[k]], bias=vcol(V_BGB + dc))
                TT(tm[k][:], bank(AO_[k]), gb[k][:], ALU.mult, [PB[AO_[k]], gbB[k]], [tmB[k]])
                TT(mg[b][:, dc, :], tm[k][:], mAt[b][:, dc, :], ALU.add, [tmB[k], mAtB[b]], [mgB[b]],
                   eng="pool", multi=True)

            def loadX(i):
                if i < 32:
                    DMA("sp", [(xt[i % 2][:], x_d[i * 128:(i + 1) * 128, :])], xtB[i % 2], (), [xtB[i % 2]])

            def stageB1(i):
                tt, j = i // 4, i % 4
                b = tt % 2
                xb_ = i % 2
                for half in range(2):
                    MM(bank(H_ + half), [(mg[b][:, kc, j * 128:(j + 1) * 128], wo[:, kc, half * 512:(half + 1) * 512])
                                         for kc in range(8)], [mgB[b], woB], [PB[H_ + half]])
                TT(hT[xb_][:], ps[:, H_ * 512:(H_ + 2) * 512], xt[xb_][:], ALU.add,
                   [PB[H_], PB[H_ + 1], xtB[xb_]], [hTB[xb_]])
                loadX(i + 2)
                DMA("sp", [(h_scr.ap()[i * 128:(i + 1) * 128, :], hT[xb_][:])], hTB[xb_], [hTB[xb_]], [hscrB],
                    multi=True)
                rms_stats(hT[xb_], hTB[xb_], i, tmp, pool_pow=True)

            loadA(0)
            loadX(0)
            loadX(1)
            loadA(1)
            for dc in range(8):
                stageA(0, dc)
            for tt in range(8):
                if tt + 2 < 8:
                    loadA(tt + 2)
                for j in range(4):
                    i = tt * 4 + j
                    if tt + 1 < 8:
                        stageA(tt + 1, 2 * j)
                        stageA(tt + 1, 2 * j + 1)
                    stageB1(i)
                    if i >= 1:
                        rms_tr(i - 1, tmp)
                    if i >= 2:
                        rms_back(i - 2, V_G2, tmp)
            rms_tr(31, tmp)
            rms_back(30, V_G2, tmp)
            rms_back(31, V_G2, tmp)
            if debug:
                DMA("sp", [(dap(u_dbg, 0, [[S_LEN, 128], [128 * S_LEN, 8], [1, S_LEN]]), uT[:])], uTB[0], uTB, [])
            S.drain_dmas()
            S.emit()
        if nphase <= 5:
            return nc

        actscrB = [Buf() for _ in range(22)]
        stW4 = ExitStack()
        wd = sb(stW4, "wd", [128, 22, 1024], BF16)
        wdB = Buf()
        actt0 = sb(stW4, "actt0", [128, 22, 512], BF16)
        actt0B = Buf()
        NPRE = 20
        with ExitStack() as st:
            wua = [sb(st, "wua%d" % i, [128, 8, 128], BF16) for i in range(2)]
            wuv = [sb(st, "wuv%d" % i, [128, 8, 128], BF16) for i in range(2)]
            wuaB, wuvB = [Buf(), Buf()], [Buf(), Buf()]
            dga = [sb(st, "dga%d" % i, [128, 3, 128], BF16) for i in range(2)]
            dgv = [sb(st, "dgv%d" % i, [128, 3, 128], BF16) for i in range(2)]
            dgaB, dgvB = [Buf(), Buf()], [Buf(), Buf()]
            ua = [sb(st, "ua%d" % i, [128, S_LEN + 2], BF16) for i in range(2)]
            uv = [sb(st, "uv%d" % i, [128, S_LEN + 2], BF16) for i in range(2)]
            uaB = [[Buf() for _ in range(8)] for _ in range(2)]
            uvB = [[Buf() for _ in range(8)] for _ in range(2)]
            sa = [sb(st, "sa%d" % i, [128, 512], F32) for i in range(2)]
            saB = [Buf(), Buf()]
            ab = [sb(st, "ab%d" % i, [128, S_LEN], BF16) for i in range(2)]
            abB = [Buf(), Buf()]
            for b in range(2):
                MEMSET(ua[b][:], 0.0, uaB[b])
                MEMSET(uv[b][:], 0.0, uvB[b])
            A_, V_, CA_, CV_ = (0, 1), (2, 3), (4, 5), (6, 7)
            for i in range(22):
                b = i % 2
                DMA("pool", [(wua[b][:], w_ua_d[i])], wuaB[b], (), [wuaB[b]])
                DMA("pool", [(wuv[b][:], w_uv_d[i])], wuvB[b], (), [wuvB[b]])
                if i == 2:
                    DMA("pool", [(wd[:, 2 * q:2 * q + 2], w_d_d[:, 2 * q:2 * q + 2]) for q in range(11)], wdB, (), [wdB])
                TT(dga[b][:], cap(ident[:, 0:1], [[0, 3], [1, 128]]),
                   cap(vcol(V_FW + i * 3), [[1, 3], [0, 128]]), ALU.mult, [identB, vecsB], [dgaB[b]])
                for step in range(10):
                    tt = step
                    if tt < 8:
                        k = tt % 2
                        tsl = slice(tt * 512, (tt + 1) * 512)
                        MM(bank(A_[k]), [(wua[b][:, kc, :], uT[:, kc, tsl]) for kc in range(8)],
                           [wuaB[b], uTB[tt]], [PB[A_[k]]])
                        MM(bank(V_[k]), [(wuv[b][:, kc, :], uT[:, kc, tsl]) for kc in range(8)],
                           [wuvB[b], uTB[tt]], [PB[V_[k]]])
                        ACT(ua[b][:, 1 + tt * 512: 1 + (tt + 1) * 512], bank(A_[k]), AF.Identity,
                            [PB[A_[k]]], [uaB[b][tt]])
                        ACT(uv[b][:, 1 + tt * 512: 1 + (tt + 1) * 512], bank(V_[k]), AF.Identity,
                            [PB[V_[k]]], [uvB[b][tt]])
                    ct = step - 2
                    if 0 <= ct < 8:
                        k = ct % 2
                        nbrs = [t for t in (ct - 1, ct, ct + 1) if 0 <= t < 8]
                        MM(bank(CA_[k]), [(dga[b][:, kk, :], ua[b][:, ct * 512 + kk: ct * 512 + kk + 512])
                                          for kk in range(3)], [dgaB[b]] + [uaB[b][t] for t in nbrs], [PB[CA_[k]]])
                        vrd = [uvB[b][t] for t in nbrs] + [vecsB]
                        TS(bank(CV_[k]), uv[b][:, ct * 512: ct * 512 + 512], vcol(V_FW + (22 + i) * 3), ALU.mult,
                           vrd, [PB[CV_[k]]])
                        for kk in (1, 2):
                            STT(bank(CV_[k]), uv[b][:, ct * 512 + kk: ct * 512 + kk + 512],
                                vcol(V_FW + (22 + i) * 3 + kk), bank(CV_[k]), ALU.mult, ALU.add,
                                vrd + [PB[CV_[k]]], [PB[CV_[k]]], chain=True)
                        ACT(sa[k][:], bank(CA_[k]), AF.Silu, [PB[CA_[k]], vecsB], [saB[k]], bias=vcol(V_FB + i))
                        STT(ab[b][:, ct * 512:(ct + 1) * 512], bank(CV_[k]), vcol(V_FB + 22 + i), sa[k][:],
                            ALU.add, ALU.mult, [PB[CV_[k]], saB[k], vecsB], [abB[b]], multi=True)
                if i == 21:
                    DMA("sp", [(actt0[:, 0:10], dap(act_scr, 0, [[S_LEN, 128], [128 * S_LEN, 10], [1, 512]])),
                               (actt0[:, 10:NPRE], dap(act_scr, 10 * 128 * S_LEN,
                                                       [[S_LEN, 128], [128 * S_LEN, NPRE - 10], [1, 512]]))],
                        actt0B, actscrB[0:NPRE], [actt0B], multi=True)
                DMA("sp", [(act_scr.ap()[i * 128:(i + 1) * 128, :], ab[b][:])], abB[b], [abB[b]], [actscrB[i]])
            S.drain_dmas()
            S.emit()
        if nphase <= 6:
            stW4.close()
            return nc

        with ExitStack() as st:
            gF = sb(st, "gFb", [128, D], F32)
            gFB = Buf()
            DMA("sp", [(gF[:], dap(gF_d.tensor, 0, [[0, 128], [1, D]]))], gFB, (), [gFB])
            actt = [actt0, sb(st, "actt1", [128, 22, 512], BF16)]
            acttB = [actt0B, Buf()]
            ht = [sb(st, "ht%d" % i, [128, D], F32) for i in range(2)]
            htB = [Buf(), Buf()]
            h2 = [sb(st, "h2%d" % i, [128, D], F32) for i in range(2)]
            h2B = [Buf(), Buf()]
            ot = [sb(st, "ot%d" % i, [128, D], F32) for i in range(2)]
            otB = [Buf(), Buf()]
            junk = sb(st, "junk5", [128, D], BF16)
            junkB = Buf()
            ssq = [sb(st, "ssq5%d" % i, [128, 1], F32) for i in range(2)]
            std = [sb(st, "std5%d" % i, [128, 1], F32) for i in range(2)]
            rstd = [sb(st, "rstd5%d" % i, [128, 1], F32) for i in range(2)]
            ssqB, stdB, rstdB = [Buf(), Buf()], [Buf(), Buf()], [Buf(), Buf()]
            lateB = Buf()

            def loadAct(tt):
                b = tt % 2
                if tt == 0:
                    DMA("sp", [(actt[0][:, NPRE:22], dap(act_scr, NPRE * 128 * S_LEN,
                                                         [[S_LEN, 128], [128 * S_LEN, 22 - NPRE], [1, 512]]))],
                        lateB, actscrB[NPRE:22], [lateB])
                    return
                if b == 0:
                    DMA("sp", [(actt[b][:, 0:11], dap(act_scr, tt * 512, [[S_LEN, 128], [128 * S_LEN, 11], [1, 512]])),
                               (actt[b][:, 11:22], dap(act_scr, 11 * 128 * S_LEN + tt * 512,
                                                       [[S_LEN, 128], [128 * S_LEN, 11], [1, 512]]))],
                        acttB[b], actscrB, [acttB[b], lateB])
                    return
                DMA("sp", [(actt[b][:, 0:11], dap(act_scr, tt * 512, [[S_LEN, 128], [128 * S_LEN, 11], [1, 512]])),
                           (actt[b][:, 11:22], dap(act_scr, 11 * 128 * S_LEN + tt * 512,
                                                   [[S_LEN, 128], [128 * S_LEN, 11], [1, 512]]))],
                    acttB[b], actscrB, [acttB[b]])

            def loadH(i):
                if i < 32:
                    DMA("sp", [(ht[i % 2][:], h_scr.ap()[i * 128:(i + 1) * 128, :])], htB[i % 2], [hscrB], [htB[i % 2]])

            loadAct(0)
            loadH(0)
            loadH(1)
            for tt in range(8):
                b = tt % 2
                if tt + 1 < 8:
                    loadAct(tt + 1)
                for j in range(4):
                    i = tt * 4 + j
                    k = i % 2
                    hb = (i % 4) * 2
                    for half in range(2):
                        prs = [(actt[b][:, kc, j * 128:(j + 1) * 128], wd[:, kc, half * 512:(half + 1) * 512])
                               for kc in range(22)]
                        if tt == 0:
                            MM(bank(hb + half), prs[:NPRE], [acttB[b], wdB], [PB[hb + half]], start=True, stop=False)
                            MM(bank(hb + half), prs[NPRE:], [lateB, wdB], [PB[hb + half]], start=False, stop=True,
                               multi=True)
                        else:
                            MM(bank(hb + half), prs, [acttB[b], wdB], [PB[hb + half]])
                    TT(h2[k][:], ps[:, hb * 512:(hb + 2) * 512], ht[k][:], ALU.add,
                       [PB[hb], PB[hb + 1], htB[k]], [h2B[k]])
                    loadH(i + 2)
                    ACT(junk[:], h2[k][:], AF.Square, [h2B[k]], [junkB, ssqB[k]], accum=ssq[k][:])
                    ACT(std[k][:], ssq[k][:], AF.Sqrt, [ssqB[k], constB], [stdB[k]], bias=eps6[:], scale=1.0 / D)
                    RECIP(rstd[k][:], std[k][:], [stdB[k]], [rstdB[k]])
                    STT(ot[k][:], h2[k][:], rstd[k][:], gF[:], ALU.mult, ALU.mult, [h2B[k], rstdB[k], gFB], [otB[k]])
                    DMA("sp", [(out_d[i * 128:(i + 1) * 128, :], ot[k][:])], otB[k], [otB[k]], [])
            S.drain_dmas()
            S.emit()
        stW4.close()
    return nc


def _prep_weights(inp):
    f = np.float32
    w_in = np.asarray(inp["w_in"], f)[0]

    def chunked(w):
        n = w.shape[1] // 128
        return np.ascontiguousarray(w.reshape(8, 128, n, 128).transpose(2, 1, 0, 3))

    def rowmaj(w):
        k = w.shape[0] // 128
        return np.ascontiguousarray(w.reshape(k, 128, w.shape[1]).transpose(1, 0, 2))

    def v128(v):
        return np.ascontiguousarray(np.asarray(v, f).reshape(-1, 128).T)

    w_up = np.asarray(inp["w_up"], f)[0]
    bg = np.asarray(inp["b_gate"], f)[0]
    cw = np.asarray(inp["conv_dw_w"], f)[0]
    fw = np.asarray(inp["ffn_dw_w"], f)[0]
    vecs = np.zeros((128, V_N), f)
    vecs[:, V_G1:V_G1 + 8] = v128(inp["norm_mix_g"][0])
    vecs[:, V_BGA:V_BGA + 8] = v128(bg[:1024])
    vecs[:, V_BGB:V_BGB + 8] = v128(bg[1024:])
    vecs[:, V_CB:V_CB + 8] = v128(inp["conv_dw_b"][0])
    vecs[:, V_LNG:V_LNG + 8] = v128(inp["conv_ln_g"][0])
    vecs[:, V_LNB:V_LNB + 8] = v128(inp["conv_ln_b"][0])
    vecs[:, V_G2:V_G2 + 8] = v128(inp["norm_ffn_g"][0])
    vecs[:, V_FB:V_FB + 44] = v128(inp["ffn_dw_b"][0])
    vecs[:, V_CW:V_CW + 248] = cw.reshape(31, 8, 128).transpose(2, 1, 0).reshape(128, 248)
    vecs[:, V_FW:V_FW + 132] = fw.reshape(3, 44, 128).transpose(2, 1, 0).reshape(128, 132)
    return {
        "w_a": chunked(w_in[:, 0:1024]),
        "w_g": chunked(w_in[:, 1024:2048]),
        "w_q": chunked(w_in[:, 2048:3584]),
        "w_k": chunked(w_in[:, 3584:5120]),
        "w_v": rowmaj(w_in[:, 5120:6656]),
        "w_gA": rowmaj(w_in[:, 6656:7680]),
        "w_gB": rowmaj(w_in[:, 7680:8704]),
        "w_co": rowmaj(np.asarray(inp["w_conv_out"], f)[0]),
        "w_ao": rowmaj(np.asarray(inp["w_attn_out"], f)[0]),
        "w_o": rowmaj(np.asarray(inp["w_out"], f)[0]),
        "w_ua": chunked(w_up[:, :2816]),
        "w_uv": chunked(w_up[:, 2816:]),
        "w_d": rowmaj(np.asarray(inp["w_down"], f)[0]),
        "vecs": vecs,
        "gF": np.ascontiguousarray(np.asarray(inp["norm_final_g"], f).reshape(1, D)),
    }


def kernel(**inputs):
    x = np.asarray(inputs["x"], np.float32)
    wts = _prep_weights(inputs)
    nc = build()
    in_maps = []
    for c in range(NCORES):
        m = dict(wts)
        m["x"] = np.ascontiguousarray(x[c])
        in_maps.append(m)
    res = run_bass_kernel_spmd(nc, in_maps, core_ids=list(range(NCORES)))
    return np.stack([np.asarray(r["out"], np.float32).reshape(S_LEN, D) for r in res.results], axis=0)
```

```python
from contextlib import ExitStack

import numpy as np
import concourse.bass as bass
import concourse.mybir as mybir
from concourse.bass_utils import run_bass_kernel_spmd

F32 = mybir.dt.float32
BF16 = mybir.dt.bfloat16
I32 = mybir.dt.int32
AF = mybir.ActivationFunctionType
ALU = mybir.AluOpType

S_LEN = 4096
D = 1024
NCORES = 8
GROUPS = ((1, 4096), (4, 1024), (16, 256))
VROW = 24 * 65
V_G1, V_BGA, V_BGB, V_CB, V_LNG, V_LNB, V_G2, V_FB, V_CW, V_FW, V_N = 0, 8, 16, 24, 32, 40, 48, 56, 100, 348, 480


class Buf:
    __slots__ = ("w", "r", "sem", "dcnt")

    def __init__(self):
        self.w = {}
        self.r = {}
        self.sem = None
        self.dcnt = 0


class Sched:
    ENGS = ("pe", "act", "dve", "pool", "sp")

    def __init__(self, nc, stack):
        self.nc = nc
        self.stack = stack
        self.semobjs = {}
        for e in self.ENGS:
            self.semobjs[("e", e)] = stack.enter_context(nc.semaphore("s_" + e))
        self.tick = {e: 0 for e in self.ENGS}
        self.seen = {e: {} for e in self.ENGS}
        self.q = {e: [] for e in self.ENGS}
        self.dma_bufs = []

    def _deps(self, eng, reads, writes, multi, chain=False):
        need = {}

        def add(d):
            for k, v in d.items():
                if need.get(k, 0) < v:
                    need[k] = v

        for b in reads:
            add(b.w)
        for b in writes:
            add(b.r)
            if not multi:
                add(b.w)
        seen = self.seen[eng]
        waits = []
        for k, v in need.items():
            if eng == "pe" and k == ("e", "pe"):
                continue
            if chain and k == ("e", eng):
                continue
            if seen.get(k, 0) >= v:
                continue
            seen[k] = v
            waits.append((k, v))
        return waits

    def _mark(self, key, val, reads, writes, multi):
        for b in reads:
            b.r[key] = val
        for b in writes:
            if multi:
                b.w[key] = val
            else:
                b.w = {key: val}
                b.r = {}

    def op(self, eng, fn, reads=(), writes=(), multi=False, chain=False):
        waits = self._deps(eng, reads, writes, multi, chain)
        self.tick[eng] += 1
        key = ("e", eng)
        self.q[eng].append((waits, fn, key))
        self._mark(key, self.tick[eng], reads, writes, multi)

    def dma(self, eng, fn, ndma, sbuf_buf, reads=(), writes=(), multi=False):
        waits = self._deps(eng, reads, writes, multi)
        key = ("d", id(sbuf_buf))
        if sbuf_buf.sem is None:
            sbuf_buf.sem = self.stack.enter_context(self.nc.semaphore("d%d" % len(self.semobjs)))
            self.semobjs[key] = sbuf_buf.sem
            self.dma_bufs.append(sbuf_buf)
        sbuf_buf.dcnt += 16 * ndma
        self.q[eng].append((waits, fn, key))
        self._mark(key, sbuf_buf.dcnt, reads, writes, multi)

    def drain_dmas(self):
        seen = self.seen["sp"]
        waits = []
        for b in self.dma_bufs:
            k = ("d", id(b))
            if seen.get(k, 0) < b.dcnt:
                seen[k] = b.dcnt
                waits.append((k, b.dcnt))
        self.q["sp"].append((waits, None, None))

    def emit(self):
        nc = self.nc
        semobjs = self.semobjs
        with nc.Block() as blk:
            for ename, deco in (("pe", blk.tensor), ("act", blk.scalar), ("dve", blk.vector),
                                ("pool", blk.gpsimd), ("sp", blk.sync)):
                items = self.q[ename]
                self.q[ename] = []
                if not items:
                    continue

                def body(e, items=items):
                    for waits, fn, key in items:
                        for k, v in waits:
                            e.wait_ge(semobjs[k], v)
                        if fn is None:
                            continue
                        if key[0] == "e":
                            fn(e).then_inc(semobjs[key], 1)
                        else:
                            fn(e, semobjs[key])

                deco(body)


def cap(base, dims):
    return bass.AP(tensor=base.tensor, offset=base.offset, ap=[list(base.ap[0])] + [list(d) for d in dims])


def build(debug=False, nphase=99):
    nc = bass.Bass("TRN2", target_bir_lowering=False)
    ext = "ExternalOutput" if debug else "Internal"

    def din(name, shape):
        return nc.dram_tensor(name, list(shape), F32, kind="ExternalInput")

    x_d = din("x", [S_LEN, D]).ap()
    w_a_d = din("w_a", [8, 128, 8, 128]).ap()
    w_g_d = din("w_g", [8, 128, 8, 128]).ap()
    w_q_d = din("w_q", [12, 128, 8, 128]).ap()
    w_k_d = din("w_k", [12, 128, 8, 128]).ap()
    w_v_d = din("w_v", [128, 8, 1536]).ap()
    w_gA_d = din("w_gA", [128, 8, 1024]).ap()
    w_gB_d = din("w_gB", [128, 8, 1024]).ap()
    w_co_d = din("w_co", [128, 8, 1024]).ap()
    w_ao_d = din("w_ao", [128, 4, 1024]).ap()
    w_o_d = din("w_o", [128, 8, 1024]).ap()
    w_ua_d = din("w_ua", [22, 128, 8, 128]).ap()
    w_uv_d = din("w_uv", [22, 128, 8, 128]).ap()
    w_d_d = din("w_d", [128, 22, 1024]).ap()
    vecs_d = din("vecs", [128, V_N]).ap()
    gF_d = din("gF", [1, D]).ap()
    out_d = nc.dram_tensor("out", [S_LEN, D], F32, kind="ExternalOutput").ap()

    y_scr = nc.dram_tensor("y_scr", [1024, S_LEN], BF16, kind=ext)
    mA_scr = nc.dram_tensor("mA_scr", [1024, S_LEN], BF16, kind=ext)
    v_scr = nc.dram_tensor("v_scr", [S_LEN, VROW], BF16, kind=ext)
    attn_scr = nc.dram_tensor("attn_scr", [512, S_LEN], BF16, kind=ext)
    h_scr = nc.dram_tensor("h_scr", [S_LEN, D], F32, kind=ext)
    act_scr = nc.dram_tensor("act_scr", [2816, S_LEN], BF16, kind=ext)
    if debug:
        u_dbg = nc.dram_tensor("u_dbg", [1024, S_LEN], BF16, kind="ExternalOutput")

    def dap(t, offset, dims):
        return bass.AP(tensor=t, offset=offset, ap=[list(d) for d in dims])

    top = ExitStack()
    with top:
        S = Sched(nc, top)

        _cnt = [0]

        def sb(st, name, shape, dt):
            _cnt[0] += 1
            return st.enter_context(nc.sbuf_tensor("t%d_%s" % (_cnt[0], name), list(shape), dt))

        def ACT(out, in_, func, reads, writes, bias=None, scale=None, accum=None, multi=False, chain=False):
            kw = {}
            if bias is not None:
                kw["bias"] = bias
            if scale is not None:
                kw["scale"] = scale
            if accum is not None:
                kw["accum_out"] = accum
            S.op("act", lambda e: e.activation(out=out, in_=in_, func=func, **kw), reads, writes, multi, chain)

        def TT(out, in0, in1, op, reads, writes, eng="dve", multi=False, chain=False):
            S.op(eng, lambda e: e.tensor_tensor(out=out, in0=in0, in1=in1, op=op), reads, writes, multi, chain)

        def TS(out, in0, s1, op0, reads, writes, s2=None, op1=None, eng="dve", multi=False):
            if op1 is None:
                S.op(eng, lambda e: e.tensor_scalar(out=out, in0=in0, scalar1=s1, scalar2=None, op0=op0),
                     reads, writes, multi)
            else:
                S.op(eng, lambda e: e.tensor_scalar(out=out, in0=in0, scalar1=s1, scalar2=s2, op0=op0, op1=op1),
                     reads, writes, multi)

        def STT(out, in0, scalar, in1, op0, op1, reads, writes, multi=False, chain=False):
            S.op("dve", lambda e: e.scalar_tensor_tensor(out=out, in0=in0, scalar=scalar, in1=in1, op0=op0, op1=op1),
                 reads, writes, multi, chain)

        def COPY(out, in_, reads, writes, eng="dve", multi=False, chain=False):
            S.op(eng, lambda e: e.tensor_copy(out=out, in_=in_), reads, writes, multi, chain)

        def RECIP(out, in_, reads, writes):
            S.op("dve", lambda e: e.reciprocal(out=out, in_=in_), reads, writes)

        def MEMSET(out, val, writes, eng="dve"):
            S.op(eng, lambda e: e.memset(out, val), (), writes)

        def MM(out, pairs, reads, writes, start=True, stop=True, multi=False):
            def fn(e):
                n = len(pairs)
                ins = None
                for i, (l, r) in enumerate(pairs):
                    ins = e.matmul(out, lhsT=l, rhs=r, start=(start and i == 0), stop=(stop and i == n - 1))
                return ins
            S.op("pe", fn, reads, writes, multi)

        def TRS(outs_ins, ident, reads, writes):
            def fn(e):
                ins = None
                for o, i in outs_ins:
                    ins = e.transpose(o, i, ident)
                return ins
            S.op("pe", fn, reads, writes)

        def DMA(eng, pairs, sbuf_buf, reads, writes, multi=False):
            def fn(e, sem):
                for o, i in pairs:
                    e.dma_start(out=o, in_=i).then_inc(sem, 16)
            S.dma(eng, fn, len(pairs), sbuf_buf, reads, writes, multi)

        ps = top.enter_context(nc.psum_tensor("ps", [128, 4096], F32))
        psb = ps.bitcast(BF16)
        PB = [Buf() for _ in range(8)]
        PH = [[Buf(), Buf()] for _ in range(8)]

        def bank(k):
            return ps[:, k * 512:(k + 1) * 512]

        uT = sb(top, "uT", [128, 8, S_LEN], BF16)
        uTB = [Buf() for _ in range(8)]
        vecs = sb(top, "vecs", [128, V_N], F32)
        vecsB = Buf()
        ident = sb(top, "ident", [128, 128], BF16)
        identB = Buf()
        eps6 = sb(top, "eps6", [128, 1], F32)
        eps5 = sb(top, "eps5", [128, 1], F32)
        neghalf = sb(top, "neghalf", [128, 1], F32)
        constB = Buf()
        iot = sb(top, "iot", [128, 448], I32)
        iotB = Buf()
        DW = sb(top, "DW", [128, 448], F32)
        VW = sb(top, "VW", [128, 448], F32)
        distB = Buf()

        def vcol(c, n=1):
            return vecs[:, c:c + n]

        DMA("sp", [(vecs[:], vecs_d)], vecsB, (), [vecsB])
        MEMSET(eps6[:], 1e-6, [constB])
        S.op("dve", lambda e: e.memset(eps5[:], 1e-5), (), [constB], multi=True)
        S.op("dve", lambda e: e.memset(neghalf[:], -0.5), (), [constB], multi=True)
        S.op("pool", lambda e: e.iota(iot[:, 0:128], [[-1, 128]], base=0, channel_multiplier=1), (), [iotB])
        COPY(DW[:, 0:128], iot[:, 0:128], [iotB], [distB])
        TS(ident[:], DW[:, 0:128], 0.0, ALU.is_equal, [distB], [identB])
        S.op("pool", lambda e: e.iota(iot[:], [[-1, 448]], base=192, channel_multiplier=1), [distB], [iotB])
        COPY(DW[:], iot[:], [iotB, identB], [distB])
        TS(VW[:], DW[:], -1.0, ALU.mult, [distB], [constB], multi=True)
        TT(DW[:], DW[:], VW[:], ALU.max, [distB, constB], [distB])
        TS(VW[:], DW[:], 64.0, ALU.is_le, [distB], [constB], multi=True)

        def rms_stats(src, srcB, i, tmp, pool_pow=False, pool_scale=False):
            junk, junkB, ssq, ssqB, std, stdB, rstd, rstdB, xs, xsB, TBs = tmp
            j = i % len(xs)
            ACT(junk[:], src[:], AF.Square, [srcB], [junkB, ssqB[j]], accum=ssq[j][:])
            if pool_pow:
                TS(std[j][:], ssq[j][:], 1.0 / D, ALU.mult, [ssqB[j]], [stdB[j]], s2=1e-6, op1=ALU.add, eng="pool")
                TT(rstd[j][:], std[j][:], neghalf[:], ALU.pow, [stdB[j], constB], [rstdB[j]], eng="pool")
            else:
                ACT(std[j][:], ssq[j][:], AF.Sqrt, [ssqB[j], constB], [stdB[j]], bias=eps6[:], scale=1.0 / D)
                RECIP(rstd[j][:], std[j][:], [stdB[j]], [rstdB[j]])
            if pool_scale:
                TS(xs[j][:], src[:], rstd[j][:], ALU.mult, [srcB, rstdB[j]], [xsB[j]], s2=1.0, op1=ALU.mult,
                   eng="pool")
            else:
                TS(xs[j][:], src[:], rstd[j][:], ALU.mult, [srcB, rstdB[j]], [xsB[j]])

        def rms_tr(i, tmp):
            xs, xsB, TBs = tmp[-3], tmp[-2], tmp[-1]
            j = i % len(xs)
            TB = TBs[i % len(TBs)]
            TRS([(psb[:, TB * 1024 + kc * 128: TB * 1024 + (kc + 1) * 128], xs[j][:, kc * 128:(kc + 1) * 128])
                 for kc in range(8)], ident[:], [xsB[j], identB], [PB[TB]])

        def rms_back(i, gcol, tmp):
            TB = tmp[-1][i % len(tmp[-1])]
            TT(uT[:, :, i * 128:(i + 1) * 128],
               cap(psb[:, TB * 1024: TB * 1024 + 1], [[128, 8], [1, 128]]),
               cap(vcol(gcol), [[1, 8], [0, 128]]), ALU.mult,
               [PB[TB], vecsB], [uTB[i // 4]], multi=True)

        def rms_tmp(st, TBs, nbuf=2):
            junk = sb(st, "junk", [128, D], BF16)
            ssq = [sb(st, "ssq%d" % i, [128, 1], F32) for i in range(nbuf)]
            std = [sb(st, "std%d" % i, [128, 1], F32) for i in range(nbuf)]
            rstd = [sb(st, "rstd%d" % i, [128, 1], F32) for i in range(nbuf)]
            xs = [sb(st, "xs%d" % i, [128, D], BF16) for i in range(nbuf)]
            mk = lambda: [Buf() for _ in range(nbuf)]
            return (junk, Buf(), ssq, mk(), std, mk(), rstd, mk(), xs, mk(), TBs)

        with ExitStack() as st:
            NX = 6
            xt = [sb(st, "xt%d" % i, [128, D], F32) for i in range(NX)]
            xtB = [Buf() for _ in range(NX)]
            tmp = rms_tmp(st, (0, 1, 2, 3), nbuf=4)
            for i in range(32 + 4):
                if i < 32:
                    DMA("sp", [(xt[i % NX][:], x_d[i * 128:(i + 1) * 128, :])], xtB[i % NX], (), [xtB[i % NX]])
                    rms_stats(xt[i % NX], xtB[i % NX], i, tmp, pool_scale=(i % 4 != 0))
                if 2 <= i < 34:
                    rms_tr(i - 2, tmp)
                if i >= 4:
                    rms_back(i - 4, V_G1, tmp)
            if debug:
                DMA("sp", [(dap(u_dbg, 0, [[S_LEN, 128], [128 * S_LEN, 8], [1, S_LEN]]), uT[:])], uTB[0], uTB, [])
            S.drain_dmas()
            S.emit()
        if nphase <= 0:
            return nc

        yscrB = [Buf() for _ in range(8)]
        stW1 = ExitStack()
        wco = sb(stW1, "wco", [128, 8, 1024], BF16)
        wgA = sb(stW1, "wgA", [128, 8, 1024], BF16)
        wcoB, wgAB = Buf(), Buf()
        wv = sb(stW1, "wv", [128, 8, 1536], BF16)
        wvB = Buf()
        vscrB = Buf()
        with ExitStack() as st:
            z = [sb(st, "z%d" % i, [128, S_LEN + 30], BF16) for i in range(2)]
            zB = [[Buf() for _ in range(8)] for _ in range(2)]
            wa = [sb(st, "wa%d" % i, [128, 8, 128], BF16) for i in range(2)]
            wg = [sb(st, "wg%d" % i, [128, 8, 128], BF16) for i in range(2)]
            waB = [Buf(), Buf()]
            wgB_ = [Buf(), Buf()]
            dg = [sb(st, "dg%d" % i, [128, 31, 128], BF16) for i in range(2)]
            dgB = [Buf(), Buf()]
            sg = [sb(st, "sg%d" % i, [128, 512], F32) for i in range(2)]
            sgB = [Buf(), Buf()]
            yb = [sb(st, "yb%d" % i, [128, S_LEN], BF16) for i in range(2)]
            ybB = [Buf(), Buf()]
            for b in range(2):
                MEMSET(z[b][:], 0.0, zB[b])
            A_, G_, Y_ = (0, 1), (2, 3), (4, 5, 6)
            NPE_TAPS = 23
            def build_dg(cc):
                bb = cc % 2
                TT(dg[bb][:], cap(ident[:, 0:1], [[0, 31], [1, 128]]),
                   cap(vcol(V_CW + cc * 31), [[1, 31], [0, 128]]), ALU.mult, [identB, vecsB], [dgB[bb]])

            for c in range(8):
                b = c % 2
                DMA("pool", [(wa[b][:], w_a_d[c])], waB[b], (), [waB[b]])
                DMA("pool", [(wg[b][:], w_g_d[c])], wgB_[b], (), [wgB_[b]])
                if c == 2:
                    DMA("pool", [(wco[:, 0:4], w_co_d[:, 0:4]), (wco[:, 4:8], w_co_d[:, 4:8])], wcoB, (), [wcoB])
                    DMA("pool", [(wgA[:, 0:4], w_gA_d[:, 0:4]), (wgA[:, 4:8], w_gA_d[:, 4:8])], wgAB, (), [wgAB])
                if c == 4:
                    DMA("pool", [(wv[:, 2 * q:2 * q + 2], w_v_d[:, 2 * q:2 * q + 2]) for q in range(4)], wvB, (), [wvB])
                if c == 0:
                    build_dg(0)
                for step in range(10):
                    tt = step
                    if step == 3 and c + 1 < 8:
                        build_dg(c + 1)
                    if tt < 8:
                        tsl = slice(tt * 512, (tt + 1) * 512)
                        MM(bank(A_[tt % 2]), [(wa[b][:, kc, :], uT[:, kc, tsl]) for kc in range(8)],
                           [waB[b], uTB[tt]], [PB[A_[tt % 2]]])
                        MM(bank(G_[tt % 2]), [(wg[b][:, kc, :], uT[:, kc, tsl]) for kc in range(8)],
                           [wgB_[b], uTB[tt]], [PB[G_[tt % 2]]])
                        ACT(sg[tt % 2][:], bank(G_[tt % 2]), AF.Sigmoid, [PB[G_[tt % 2]]], [sgB[tt % 2]])
                        TT(z[b][:, 15 + tt * 512: 15 + (tt + 1) * 512], bank(A_[tt % 2]), sg[tt % 2][:], ALU.mult,
                           [PB[A_[tt % 2]], sgB[tt % 2]], [zB[b][tt]])
                    ct = step - 2
                    if 0 <= ct < 8:
                        rd = [dgB[b]] + [zB[b][t] for t in (ct - 1, ct, ct + 1) if 0 <= t < 8]
                        yk = Y_[ct % 3]
                        npe = 31 if (c == 7 and ct >= 6) else NPE_TAPS
                        MM(bank(yk),
                           [(dg[b][:, k, :], z[b][:, ct * 512 + k: ct * 512 + k + 512]) for k in range(npe)],
                           rd, [PB[yk]])
                        for k in range(npe, 31):
                            STT(bank(yk), z[b][:, ct * 512 + k: ct * 512 + k + 512], vcol(V_CW + c * 31 + k),
                                bank(yk), ALU.mult, ALU.add, rd[1:] + [PB[yk], vecsB], [PB[yk]],
                                chain=(k > npe))
                        ACT(yb[b][:, ct * 512:(ct + 1) * 512], bank(yk), AF.Identity,
                            [PB[yk], vecsB], [ybB[b]], bias=vcol(V_CB + c), multi=True)
                DMA("sp", [(y_scr.ap()[c * 128:(c + 1) * 128, :], yb[b][:])], ybB[b], [ybB[b]], [yscrB[c]])
            S.drain_dmas()
            S.emit()
        if nphase <= 1:
            stW1.close()
            return nc

        mAscrB = [Buf() for _ in range(8)]
        with ExitStack() as st:
            onesb = sb(st, "onesb", [128, 128], BF16)
            onesB = Buf()
            MEMSET(onesb[:], 1.0 / 1024.0, [onesB])
            yt = [sb(st, "yt%d" % i, [128, 8, 512], BF16) for i in range(2)]
            ytB = [Buf(), Buf()]
            mean1 = sb(st, "mean", [128, 512], F32)
            var = sb(st, "var", [128, 512], F32)
            rstd1 = sb(st, "rstdl", [128, 512], F32)
            mean, rstd = [mean1, mean1], [rstd1, rstd1]
            msq = var
            mB_, rB_, vB_ = Buf(), Buf(), Buf()
            meanB, msqB, varB, rstdB = [mB_, mB_], vB_, vB_, [rB_, rB_]
            vt = [sb(st, "vt%d" % i, [128, 24, 65], BF16) for i in range(2)]
            vtB = [Buf(), Buf()]
            for b in range(2):
                MEMSET(vt[b][:], 1.0, [vtB[b]])
            VB_ = (6, 7)
            vctr = [0]

            def vproj(tt):
                for i in range(4 * tt, 4 * tt + 4):
                    b = i % 2
                    for n in range(3):
                        bk = VB_[vctr[0] % 2]
                        vctr[0] += 1
                        MM(bank(bk), [(uT[:, kc, i * 128:(i + 1) * 128], wv[:, kc, n * 512:(n + 1) * 512])
                                      for kc in range(8)], [uTB[i // 4], wvB], [PB[bk]])
                        o_ap = vt[b][:, n * 8:(n + 1) * 8, 0:64]
                        i_ap = cap(ps[:, bk * 512: bk * 512 + 1], [[64, 8], [1, 64]])
                        COPY(o_ap, i_ap, [PB[bk]], [vtB[b]], multi=True)
                    DMA("sp", [(dap(v_scr, i * 128 * VROW, [[VROW, 128], [65, 24], [1, 65]]), vt[b][:])],
                        vtB[b], [vtB[b]], [vscrB], multi=True)
            tnF = sb(st, "tnF", [128, 8, 512], F32)
            tnFB = Buf()
            s_ = sb(st, "s_", [128, 8, 512], BF16)
            sB = [Buf() for _ in range(8)]
            ga = [sb(st, "ga%d" % i, [128, 512], F32) for i in range(2)]
            gaB = [Buf(), Buf()]
            mA1 = sb(st, "mA", [128, 8, 512], BF16)
            mA = [mA1, mA1]
            mAB1 = Buf()
            mAB = [mAB1, mAB1]
            ysq = sb(st, "ysq", [128, 8, 512], BF16)
            ysqB = Buf()
            s2 = sb(st, "s_2", [128, 8, 512], BF16)
            sT = [s_, s2]
            sBB = [sB, [Buf() for _ in range(8)]]
            M_, Q_, C_, GA_ = 0, 1, (2, 3), (4, 5)

            def load1b(tt):
                b = tt % 2
                DMA("sp", [(yt[b][:], dap(y_scr, tt * 512, [[S_LEN, 128], [128 * S_LEN, 8], [1, 512]]))],
                    ytB[b], yscrB, [ytB[b]])

            def stats1b(tt):
                b = tt % 2
                TT(ysq[:], yt[b][:], yt[b][:], ALU.mult, [ytB[b]], [ysqB], eng=("dve" if tt == 0 else "pool"))
                MM(bank(M_), [(onesb[:], yt[b][:, c, :]) for c in range(8)], [onesB, ytB[b]], [PB[M_]])
                MM(bank(Q_), [(onesb[:], ysq[:, c, :]) for c in range(8)], [onesB, ysqB], [PB[Q_]])
                COPY(mean[b][:], bank(M_), [PB[M_]], [meanB[b]])
                TT(msq[:], mean[b][:], mean[b][:], ALU.mult, [meanB[b]], [msqB])
                TT(var[:], bank(Q_), msq[:], ALU.subtract, [PB[Q_], msqB], [varB])
                ACT(var[:], var[:], AF.Ln, [varB, constB], [varB], bias=eps5[:], scale=1.0)
                ACT(rstd[b][:], var[:], AF.Exp, [varB], [rstdB[b]], scale=-0.5, chain=True)
                TT(tnF[:], yt[b][:], cap(mean[b][:, 0:1], [[0, 8], [1, 512]]), ALU.subtract,
                   [ytB[b], meanB[b]], [tnFB], eng="pool")
                TT(tnF[:], tnF[:], cap(rstd[b][:, 0:1], [[0, 8], [1, 512]]), ALU.mult,
                   [tnFB, rstdB[b]], [tnFB], eng="pool", chain=True)

            def silu1b(tt):
                b = tt % 2
                for c in range(8):
                    ACT(sT[b][:, c, :], tnF[:, c, :], AF.Silu, [tnFB, vecsB], [sBB[b][c]],
                        bias=vcol(V_LNB + c), scale=vcol(V_LNG + c))

            def back1b(tt, dc):
                b = tt % 2
                tsl = slice(tt * 512, (tt + 1) * 512)
                k = dc % 2
                dsl = slice(dc * 128, (dc + 1) * 128)
                MM(bank(C_[k]), [(wco[:, kc, dsl], sT[b][:, kc, :]) for kc in range(8)], [wcoB] + sBB[b],
                   [PB[C_[k]]])
                MM(bank(GA_[k]), [(wgA[:, kc, dsl], uT[:, kc, tsl]) for kc in range(8)],
                   [wgAB, uTB[tt]], [PB[GA_[k]]])
                ACT(ga[k][:], bank(GA_[k]), AF.Sigmoid, [PB[GA_[k]], vecsB], [gaB[k]], bias=vcol(V_BGA + dc))
                TT(mA[b][:, dc, :], bank(C_[k]), ga[k][:], ALU.mult, [PB[C_[k]], gaB[k]], [mAB[b]], multi=True)

            def store1b(tt):
                b = tt % 2
                DMA("sp", [(dap(mA_scr, tt * 512, [[S_LEN, 128], [128 * S_LEN, 8], [1, 512]]), mA[b][:])],
                    mAB[b], [mAB[b]], [mAscrB[tt]])

            load1b(0)
            load1b(1)
            for tt in range(9):
                if tt < 8:
                    stats1b(tt)
                    vproj(tt)
                if tt >= 1:
                    for dc in range(8):
                        back1b(tt - 1, dc)
                    store1b(tt - 1)
                if tt < 8:
                    silu1b(tt)
                if tt + 2 < 8:
                    load1b(tt + 2)
            S.drain_dmas()
            S.emit()
        stW1.close()
        if nphase <= 2:
            return nc

        attnscrB = [Buf() for _ in range(4)]
        with ExitStack() as st:
            qT = [sb(st, "qT%d" % i, [128, S_LEN], BF16) for i in range(2)]
            kT = [sb(st, "kT%d" % i, [128, S_LEN], BF16) for i in range(2)]
            qTB, kTB = [Buf(), Buf()], [Buf(), Buf()]
            vb = [sb(st, "vb%d" % i, [128, 36, 2, 128], BF16) for i in range(2)]
            vbB = [Buf(), Buf()]
            wq = [sb(st, "wq%d" % i, [128, 8, 128], BF16) for i in range(2)]
            wk = [sb(st, "wk%d" % i, [128, 8, 128], BF16) for i in range(2)]
            wqB, wkB = [Buf(), Buf()], [Buf(), Buf()]
            E2 = [sb(st, "E2%d" % i, [128, 2, 256], BF16) for i in range(2)]
            E2B = [Buf(), Buf()]
            Ef = sb(st, "Ef", [128, 256], F32)
            EfB = Buf()
            E0 = [sb(st, "E0%d" % i, [128, 2, 128], BF16) for i in range(2)]
            E2b = [sb(st, "E2b%d" % i, [128, 2, 256], BF16) for i in range(2)]
            Ef0 = sb(st, "Ef0", [128, 128], F32)
            Ef0B = Buf()
            osum = sb(st, "osum", [128, 2, S_LEN], F32)
            osumB = [Buf(), Buf()]
            at = sb(st, "at", [128, S_LEN], BF16)
            atB = Buf()
            NSLOT = 3
            LAG = 2
            pT = [sb(st, "pT%d" % i, [128, 512], BF16) for i in range(NSLOT)]
            pTB = [Buf() for _ in range(NSLOT)]
            rl = sb(st, "rl", [128, 512], F32)
            rlB = Buf()
            rq = [sb(st, "rq%d" % i, [64, 512], F32) for i in range(2)]
            rqB = [Buf(), Buf()]
            QK_ = (0, 1)
            SBK = ((0, 1), (2, 3), (4, 5))
            OB_ = (6, 7)
            seq = [(p, g) for p in range(3) for g in range(3)] + [(3, 2), (3, 1), (3, 0)]

            def proj(n, hook=None):
                p, g = seq[n]
                r, L = GROUPS[g]
                b = n % 2
                nb = L // 128
                DMA("pool", [(wq[b][:], w_q_d[4 * g + p])], wqB[b], (), [wqB[b]])
                DMA("pool", [(wk[b][:], w_k_d[4 * g + p])], wkB[b], (), [wkB[b]])
                if n < 2:
                    MEMSET(vb[b][:], 1.0, [vbB[b]], eng="pool")
                for h in range(2):
                    head = 8 * g + 2 * p + h
                    slope = 2.0 ** (-(head + 1) / 3.0)
                    if g == 2:
                        views = ((E2[b], 192), (E2b[b], 64))
                    else:
                        views = ((E2[b], 128),)
                    for (Et, c0) in views:
                        ACT(Ef[:], DW[:, c0:c0 + 256], AF.Exp, [distB], [EfB], scale=-slope * r)
                        TT(Et[:, h, :], Ef[:], VW[:, c0:c0 + 256], ALU.mult, [EfB, constB], [E2B[b]], multi=True)
                    if g != 2:
                        ACT(Ef0[:], DW[:, 192:320], AF.Exp, [distB], [Ef0B], scale=-slope * r)
                        TT(E0[b][:, h, :], Ef0[:], VW[:, 192:320], ALU.mult, [Ef0B, constB], [E2B[b]], multi=True)
                pairs = []
                hc = (8 * g + 2 * p) * 65
                for h in range(2):
                    hh = hc + 65 * h
                    if g == 2:
                        for c in range(r):
                            pairs.append((vb[b][:, 2 * c: 2 * c + 2, h, 0:64],
                                          dap(v_scr, c * VROW + hh, [[r * VROW, 128], [128 * r * VROW, 2], [1, 64]])))
                    else:
                        for c in range(r):
                            base = c * (nb + 1)
                            if nb > 1:
                                pairs.append((vb[b][:, base + 1: base + nb, h, 0:64],
                                              dap(v_scr, (64 * r + c) * VROW + hh,
                                                  [[r * VROW, 128], [128 * r * VROW, nb - 1], [1, 64]])))
                            pairs.append((vb[b][0:64, base, h, 0:64],
                                          dap(v_scr, c * VROW + hh, [[r * VROW, 64], [1, 64]])))
                            pairs.append((vb[b][0:64, base + nb, h, 0:64],
                                          dap(v_scr, ((L - 64) * r + c) * VROW + hh, [[r * VROW, 64], [1, 64]])))
                DMA("sp", pairs, vbB[b], [vscrB], [vbB[b]])
                ni = 512 // r
                for tt in range(8):
                    tsl = slice(tt * 512, (tt + 1) * 512)
                    MM(bank(QK_[0]), [(wq[b][:, kc, :], uT[:, kc, tsl]) for kc in range(8)],
                       [wqB[b], uTB[tt]], [PB[QK_[0]]])
                    MM(bank(QK_[1]), [(wk[b][:, kc, :], uT[:, kc, tsl]) for kc in range(8)],
                       [wkB[b], uTB[tt]], [PB[QK_[1]]])
                    if r == 1:
                        qo, ko = qT[b][:, tsl], kT[b][:, tsl]
                        qi, ki = bank(QK_[0]), bank(QK_[1])
                    else:
                        qo = cap(qT[b][:, tt * ni: tt * ni + 1], [[L, r], [1, ni]])
                        ko = cap(kT[b][:, tt * ni: tt * ni + 1], [[L, r], [1, ni]])
                        qi = cap(ps[:, QK_[0] * 512: QK_[0] * 512 + 1], [[1, r], [r, ni]])
                        ki = cap(ps[:, QK_[1] * 512: QK_[1] * 512 + 1], [[1, r], [r, ni]])
                    if hook is not None:
                        TS(qo, qi, 0.125, ALU.mult, [PB[QK_[0]]], [qTB[b]], multi=True)
                    else:
                        ACT(qo, qi, AF.Identity, [PB[QK_[0]]], [qTB[b]], scale=0.125, multi=True)
                    COPY(ko, ki, [PB[QK_[1]]], [kTB[b]], multi=True)
                    if hook is not None:
                        hook(tt)

            blk_ctr = [0]

            def attn(n):
                p, g = seq[n]
                first = (n % 3 == 0)
                inline_norm = (n == len(seq) - 1)
                r, L = GROUPS[g]
                b = n % 2
                nb = L // 128
                blocks = []
                for c in range(r):
                    for kb in range(nb + 1):
                        blocks.append((c, kb, blk_ctr[0] % NSLOT))
                        blk_ctr[0] += 1

                def geom(c, kb):
                    if kb == 0:
                        return 64, c * L, c * L, 128
                    if kb == nb:
                        return 64, c * L + L - 64, c * L + L - 128, 128
                    return 128, c * L + 128 * kb - 64, c * L + 128 * (kb - 1), 256

                def front(c, kb, sl):
                    KP, koff, qoff, N = geom(c, kb)
                    bks = SBK[sl]
                    for h in range(2):
                        hp = slice(64 * h, 64 * h + 64)
                        MM(ps[0:KP, bks[h] * 512: bks[h] * 512 + N],
                           [(kT[b][hp, koff:koff + KP], qT[b][hp, qoff:qoff + N])],
                           [kTB[b], qTB[b]], [PB[bks[h]]])
                    src = cap(ps[0:KP, bks[0] * 512: bks[0] * 512 + 1], [[512, 2], [1, N]])
                    ptv = cap(pT[sl][0:KP, 0:1], [[N, 2], [1, N]])
                    ACT(ptv, src, AF.Exp, [PB[bks[0]], PB[bks[1]]], [pTB[sl]])
                    ee = E0[b][0:64, :, 0:128] if kb == 0 else E2[b][0:KP, :, 0:N]
                    TT(ptv, ptv, ee, ALU.mult, [pTB[sl], E2B[b]], [pTB[sl]])

                def pv(out, l, r_, reads, bkb, start):
                    S.op("pe", lambda e: e.matmul(out, lhsT=l, rhs=r_, start=start, stop=True, skip_group_check=True),
                         reads, [PB[bkb]], True)

                pend_norm = []

                def back(c, kb, sl):
                    KP, koff, qoff, N = geom(c, kb)
                    base = c * (nb + 1)
                    if pend_norm and kb % 2 == 1:
                        norm_head(p, pend_norm.pop(), 1)
                    for h in range(2):
                        vl = vb[b][0:KP, base + kb, h, :]
                        if kb >= 1:
                            nq = kb - 1
                            bk = OB_[(nq // 2) % 2]
                            col = bk * 512 + h * 256 + (nq % 2) * 128
                            pv(ps[0:128, col:col + 128], vl, pT[sl][0:KP, h * N: h * N + 128],
                               [vbB[b], pTB[sl]], bk, False)
                        if kb <= nb - 1:
                            nq = kb
                            bk = OB_[(nq // 2) % 2]
                            col = bk * 512 + h * 256 + (nq % 2) * 128
                            pv(ps[0:128, col:col + 128], vl, pT[sl][0:KP, h * N + N - 128: h * N + N],
                               [vbB[b], pTB[sl]], bk, (nq % 2 == 0 and h == 0))
                    if kb >= 2 and kb % 2 == 0:
                        G2 = kb // 2 - 1
                        bk = OB_[G2 % 2]
                        src = cap(ps[0:128, bk * 512: bk * 512 + 1], [[256, 2], [1, 256]])
                        dst = cap(osum[0:128, 0, 256 * G2 * r + c: 256 * G2 * r + c + 1], [[S_LEN, 2], [r, 256]])
                        if first:
                            COPY(dst, src, [PB[bk]], [osumB[0]], multi=True, chain=True)
                        else:
                            TT(dst, dst, src, ALU.add, [PB[bk], osumB[0]], [osumB[0]], multi=True, chain=True)
                        if inline_norm and G2 % 2 == 1:
                            norm_head(p, G2 // 2, 0)
                            pend_norm.append(G2 // 2)

                def front2(c, j, sl):
                    bks = SBK[sl]
                    for h in range(2):
                        hp = slice(64 * h, 64 * h + 64)
                        MM(ps[:, bks[h] * 512: bks[h] * 512 + 256],
                           [(kT[b][hp, c * L + 128 * j: c * L + 128 * j + 128], qT[b][hp, c * L: c * L + 256])],
                           [kTB[b], qTB[b]], [PB[bks[h]]])
                    src = cap(ps[:, bks[0] * 512: bks[0] * 512 + 1], [[512, 2], [1, 256]])
                    ACT(pT[sl][:], src, AF.Exp, [PB[bks[0]], PB[bks[1]]], [pTB[sl]])
                    Et = E2[b] if j == 0 else E2b[b]
                    TT(pT[sl][:], pT[sl][:], Et[:], ALU.mult, [pTB[sl], E2B[b]], [pTB[sl]])

                def back2(c, j, sl):
                    bk = OB_[c % 2]
                    for h in range(2):
                        pv(ps[0:128, bk * 512 + h * 256: bk * 512 + h * 256 + 256],
                           vb[b][:, 2 * c + j, h, :], pT[sl][:, h * 256:(h + 1) * 256],
                           [vbB[b], pTB[sl]], bk, (j == 0 and h == 0))
                    if j == 1:
                        src = ps[0:128, bk * 512:(bk + 1) * 512]
                        dst = cap(osum[0:128, 0, c:c + 1], [[S_LEN, 2], [r, 256]])
                        if first:
                            COPY(dst, src, [PB[bk]], [osumB[0]], multi=True, chain=True)
                        else:
                            TT(dst, dst, src, ALU.add, [PB[bk], osumB[0]], [osumB[0]], multi=True, chain=True)

                if g == 2:
                    blocks = []
                    for c in range(r):
                        for j in range(2):
                            blocks.append((c, j, blk_ctr[0] % NSLOT))
                            blk_ctr[0] += 1
                    ff, bb = front2, back2
                else:
                    ff, bb = front, back
                nblk = len(blocks)
                for i in range(nblk + LAG):
                    if i < nblk:
                        ff(*blocks[i])
                    if i - LAG >= 0:
                        bb(*blocks[i - LAG])
                while pend_norm:
                    norm_head(p, pend_norm.pop(), 1)

            def norm_head(p, tt, h):
                tsl = slice(tt * 512, (tt + 1) * 512)
                ACT(rl[64:128, :], osum[64:128, h, tsl], AF.Ln, [osumB[0]], [rlB])
                ACT(rq[h][0:64, :], rl[64:128, :], AF.Exp, [rlB], [rqB[h]], scale=-1.0, chain=True)
                TT(at[64 * h:64 * h + 64, tsl], osum[0:64, h, tsl], rq[h][0:64, :], ALU.mult,
                   [osumB[0], rqB[h]], [atB], multi=True)

            def norm_tile(p, tt):
                tsl = slice(tt * 512, (tt + 1) * 512)
                for h in range(2):
                    ACT(rl[64:128, :], osum[64:128, h, tsl], AF.Ln, [osumB[0]], [rlB])
                    ACT(rq[h][0:64, :], rl[64:128, :], AF.Exp, [rlB], [rqB[h]], scale=-1.0, chain=True)
                    TT(at[64 * h:64 * h + 64, tsl], osum[0:64, h, tsl], rq[h][0:64, :], ALU.mult,
                       [osumB[0], rqB[h]], [atB], multi=True)

            def norm_store(p):
                DMA("sp", [(attn_scr.ap()[p * 128:(p + 1) * 128, :], at[:])], atB, [atB], [attnscrB[p]])

            pend = None
            proj(0)
            for n in range(len(seq)):
                if n + 1 < len(seq):
                    if pend is not None:
                        pp = pend
                        proj(n + 1, hook=lambda tt: norm_tile(pp, tt))
                        norm_store(pp)
                        pend = None
                    else:
                        proj(n + 1)
                attn(n)
                if n % 3 == 2 and n != len(seq) - 1:
                    pend = seq[n][0]
            norm_store(3)
            S.drain_dmas()
            S.emit()
        if nphase <= 4:
            return nc

        hscrB = Buf()
        with ExitStack() as st:
            wao = sb(st, "wao", [128, 4, 1024], BF16)
            wgBt = sb(st, "wgBt", [128, 8, 1024], BF16)
            wo = sb(st, "wo", [128, 8, 1024], BF16)
            waoB, wgBB, woB = Buf(), Buf(), Buf()
            DMA("pool", [(wgBt[:, 0:4], w_gB_d[:, 0:4]), (wgBt[:, 4:8], w_gB_d[:, 4:8])], wgBB, (), [wgBB])
            DMA("pool", [(wao[:, 0:2], w_ao_d[:, 0:2]), (wao[:, 2:4], w_ao_d[:, 2:4])], waoB, (), [waoB])
            DMA("pool", [(wo[:, 0:4], w_o_d[:, 0:4]), (wo[:, 4:8], w_o_d[:, 4:8])], woB, (), [woB])
            mAt = [sb(st, "mAt%d" % i, [128, 8, 512], BF16) for i in range(2)]
            att = [sb(st, "att%d" % i, [128, 4, 512], BF16) for i in range(2)]
            mAtB, attB = [Buf(), Buf()], [Buf(), Buf()]
            mg = [sb(st, "mg%d" % i, [128, 8, 512], BF16) for i in range(2)]
            mgB = [Buf(), Buf()]
            gb = [sb(st, "gb%d" % i, [128, 512], F32) for i in range(2)]
            gbB = [Buf(), Buf()]
            tm = [sb(st, "tm%d" % i, [128, 512], F32) for i in range(2)]
            tmB = [Buf(), Buf()]
            xt = [sb(st, "xt3%d" % i, [128, D], F32) for i in range(2)]
            xtB = [Buf(), Buf()]
            hT = [sb(st, "hT%d" % i, [128, D], F32) for i in range(2)]
            hTB = [Buf(), Buf()]
            tmp = rms_tmp(st, (6, 7))
            AO_, GB_, H_ = (0, 1), (2, 3), 4

            def loadA(tt):
                b = tt % 2
                DMA("sp", [(mAt[b][:], dap(mA_scr, tt * 512, [[S_LEN, 128], [128 * S_LEN, 8], [1, 512]]))],
                    mAtB[b], [mAscrB[tt]], [mAtB[b]])
                DMA("sp", [(att[b][:], dap(attn_scr, tt * 512, [[S_LEN, 128], [128 * S_LEN, 4], [1, 512]]))],
                    attB[b], attnscrB, [attB[b]])

            def stageA(tt, dc):
                b = tt % 2
                tsl = slice(tt * 512, (tt + 1) * 512)
                k = dc % 2
                dsl = slice(dc * 128, (dc + 1) * 128)
                MM(bank(GB_[k]), [(wgBt[:, kc, dsl], uT[:, kc, tsl]) for kc in range(8)],
                   [wgBB, uTB[tt]], [PB[GB_[k]]])
                MM(bank(AO_[k]), [(wao[:, kc, dsl], att[b][:, kc, :]) for kc in range(4)],
                   [waoB, attB[b]], [PB[AO_[k]]])
                ACT(gb[k][:], bank(GB_[k]), AF.Sigmoid, [PB[GB_[k]], vecsB], [gbB[k]], bias=vcol(V_BGB + dc))
                TT(tm[k][:], bank(AO_[k]), gb[k][:], ALU.mult, [PB[AO_[k]], gbB[k]], [tmB[k]])
                TT(mg[b][:, dc, :], tm[k][:], mAt[b][:, dc, :], ALU.add, [tmB[k], mAtB[b]], [mgB[b]],
                   eng="pool", multi=True)

            def loadX(i):
                if i < 32:
                    DMA("sp", [(xt[i % 2][:], x_d[i * 128:(i + 1) * 128, :])], xtB[i % 2], (), [xtB[i % 2]])

            def stageB1(i):
                tt, j = i // 4, i % 4
                b = tt % 2
                xb_ = i % 2
                for half in range(2):
                    MM(bank(H_ + half), [(mg[b][:, kc, j * 128:(j + 1) * 128], wo[:, kc, half * 512:(half + 1) * 512])
                                         for kc in range(8)], [mgB[b], woB], [PB[H_ + half]])
                TT(hT[xb_][:], ps[:, H_ * 512:(H_ + 2) * 512], xt[xb_][:], ALU.add,
                   [PB[H_], PB[H_ + 1], xtB[xb_]], [hTB[xb_]])
                loadX(i + 2)
                DMA("sp", [(h_scr.ap()[i * 128:(i + 1) * 128, :], hT[xb_][:])], hTB[xb_], [hTB[xb_]], [hscrB],
                    multi=True)
                rms_stats(hT[xb_], hTB[xb_], i, tmp, pool_pow=True)

            loadA(0)
            loadX(0)
            loadX(1)
            loadA(1)
            for dc in range(8):
                stageA(0, dc)
            for tt in range(8):
                if tt + 2 < 8:
                    loadA(tt + 2)
                for j in range(4):
                    i = tt * 4 + j
                    if tt + 1 < 8:
                        stageA(tt + 1, 2 * j)
                        stageA(tt + 1, 2 * j + 1)
                    stageB1(i)
                    if i >= 1:
                        rms_tr(i - 1, tmp)
                    if i >= 2:
                        rms_back(i - 2, V_G2, tmp)
            rms_tr(31, tmp)
            rms_back(30, V_G2, tmp)
            rms_back(31, V_G2, tmp)
            if debug:
                DMA("sp", [(dap(u_dbg, 0, [[S_LEN, 128], [128 * S_LEN, 8], [1, S_LEN]]), uT[:])], uTB[0], uTB, [])
            S.drain_dmas()
            S.emit()
        if nphase <= 5:
            return nc

        actscrB = [Buf() for _ in range(22)]
        stW4 = ExitStack()
        wd = sb(stW4, "wd", [128, 22, 1024], BF16)
        wdB = Buf()
        actt0 = sb(stW4, "actt0", [128, 22, 512], BF16)
        actt0B = Buf()
        NPRE = 20
        with ExitStack() as st:
            wua = [sb(st, "wua%d" % i, [128, 8, 128], BF16) for i in range(2)]
            wuv = [sb(st, "wuv%d" % i, [128, 8, 128], BF16) for i in range(2)]
            wuaB, wuvB = [Buf(), Buf()], [Buf(), Buf()]
            dga = [sb(st, "dga%d" % i, [128, 3, 128], BF16) for i in range(2)]
            dgv = [sb(st, "dgv%d" % i, [128, 3, 128], BF16) for i in range(2)]
            dgaB, dgvB = [Buf(), Buf()], [Buf(), Buf()]
            ua = [sb(st, "ua%d" % i, [128, S_LEN + 2], BF16) for i in range(2)]
            uv = [sb(st, "uv%d" % i, [128, S_LEN + 2], BF16) for i in range(2)]
            uaB = [[Buf() for _ in range(8)] for _ in range(2)]
            uvB = [[Buf() for _ in range(8)] for _ in range(2)]
            sa = [sb(st, "sa%d" % i, [128, 512], F32) for i in range(2)]
            saB = [Buf(), Buf()]
            ab = [sb(st, "ab%d" % i, [128, S_LEN], BF16) for i in range(2)]
            abB = [Buf(), Buf()]
            for b in range(2):
                MEMSET(ua[b][:], 0.0, uaB[b])
                MEMSET(uv[b][:], 0.0, uvB[b])
            A_, V_, CA_, CV_ = (0, 1), (2, 3), (4, 5), (6, 7)
            for i in range(22):
                b = i % 2
                DMA("pool", [(wua[b][:], w_ua_d[i])], wuaB[b], (), [wuaB[b]])
                DMA("pool", [(wuv[b][:], w_uv_d[i])], wuvB[b], (), [wuvB[b]])
                if i == 2:
                    DMA("pool", [(wd[:, 2 * q:2 * q + 2], w_d_d[:, 2 * q:2 * q + 2]) for q in range(11)], wdB, (), [wdB])
                TT(dga[b][:], cap(ident[:, 0:1], [[0, 3], [1, 128]]),
                   cap(vcol(V_FW + i * 3), [[1, 3], [0, 128]]), ALU.mult, [identB, vecsB], [dgaB[b]])
                for step in range(10):
                    tt = step
                    if tt < 8:
                        k = tt % 2
                        tsl = slice(tt * 512, (tt + 1) * 512)
                        MM(bank(A_[k]), [(wua[b][:, kc, :], uT[:, kc, tsl]) for kc in range(8)],
                           [wuaB[b], uTB[tt]], [PB[A_[k]]])
                        MM(bank(V_[k]), [(wuv[b][:, kc, :], uT[:, kc, tsl]) for kc in range(8)],
                           [wuvB[b], uTB[tt]], [PB[V_[k]]])
                        ACT(ua[b][:, 1 + tt * 512: 1 + (tt + 1) * 512], bank(A_[k]), AF.Identity,
                            [PB[A_[k]]], [uaB[b][tt]])
                        ACT(uv[b][:, 1 + tt * 512: 1 + (tt + 1) * 512], bank(V_[k]), AF.Identity,
                            [PB[V_[k]]], [uvB[b][tt]])
                    ct = step - 2
                    if 0 <= ct < 8:
                        k = ct % 2
                        nbrs = [t for t in (ct - 1, ct, ct + 1) if 0 <= t < 8]
                        MM(bank(CA_[k]), [(dga[b][:, kk, :], ua[b][:, ct * 512 + kk: ct * 512 + kk + 512])
                                          for kk in range(3)], [dgaB[b]] + [uaB[b][t] for t in nbrs], [PB[CA_[k]]])
                        vrd = [uvB[b][t] for t in nbrs] + [vecsB]
                        TS(bank(CV_[k]), uv[b][:, ct * 512: ct * 512 + 512], vcol(V_FW + (22 + i) * 3), ALU.mult,
                           vrd, [PB[CV_[k]]])
                        for kk in (1, 2):
                            STT(bank(CV_[k]), uv[b][:, ct * 512 + kk: ct * 512 + kk + 512],
                                vcol(V_FW + (22 + i) * 3 + kk), bank(CV_[k]), ALU.mult, ALU.add,
                                vrd + [PB[CV_[k]]], [PB[CV_[k]]], chain=True)
                        ACT(sa[k][:], bank(CA_[k]), AF.Silu, [PB[CA_[k]], vecsB], [saB[k]], bias=vcol(V_FB + i))
                        STT(ab[b][:, ct * 512:(ct + 1) * 512], bank(CV_[k]), vcol(V_FB + 22 + i), sa[k][:],
                            ALU.add, ALU.mult, [PB[CV_[k]], saB[k], vecsB], [abB[b]], multi=True)
                if i == 21:
                    DMA("sp", [(actt0[:, 0:10], dap(act_scr, 0, [[S_LEN, 128], [128 * S_LEN, 10], [1, 512]])),
                               (actt0[:, 10:NPRE], dap(act_scr, 10 * 128 * S_LEN,
                                                       [[S_LEN, 128], [128 * S_LEN, NPRE - 10], [1, 512]]))],
                        actt0B, actscrB[0:NPRE], [actt0B], multi=True)
                DMA("sp", [(act_scr.ap()[i * 128:(i + 1) * 128, :], ab[b][:])], abB[b], [abB[b]], [actscrB[i]])
            S.drain_dmas()
            S.emit()
        if nphase <= 6:
            stW4.close()
            return nc

        with ExitStack() as st:
            gF = sb(st, "gFb", [128, D], F32)
            gFB = Buf()
            DMA("sp", [(gF[:], dap(gF_d.tensor, 0, [[0, 128], [1, D]]))], gFB, (), [gFB])
            actt = [actt0, sb(st, "actt1", [128, 22, 512], BF16)]
            acttB = [actt0B, Buf()]
            ht = [sb(st, "ht%d" % i, [128, D], F32) for i in range(2)]
            htB = [Buf(), Buf()]
            h2 = [sb(st, "h2%d" % i, [128, D], F32) for i in range(2)]
            h2B = [Buf(), Buf()]
            ot = [sb(st, "ot%d" % i, [128, D], F32) for i in range(2)]
            otB = [Buf(), Buf()]
            junk = sb(st, "junk5", [128, D], BF16)
            junkB = Buf()
            ssq = [sb(st, "ssq5%d" % i, [128, 1], F32) for i in range(2)]
            std = [sb(st, "std5%d" % i, [128, 1], F32) for i in range(2)]
            rstd = [sb(st, "rstd5%d" % i, [128, 1], F32) for i in range(2)]
            ssqB, stdB, rstdB = [Buf(), Buf()], [Buf(), Buf()], [Buf(), Buf()]
            lateB = Buf()

            def loadAct(tt):
                b = tt % 2
                if tt == 0:
                    DMA("sp", [(actt[0][:, NPRE:22], dap(act_scr, NPRE * 128 * S_LEN,
                                                         [[S_LEN, 128], [128 * S_LEN, 22 - NPRE], [1, 512]]))],
                        lateB, actscrB[NPRE:22], [lateB])
                    return
                if b == 0:
                    DMA("sp", [(actt[b][:, 0:11], dap(act_scr, tt * 512, [[S_LEN, 128], [128 * S_LEN, 11], [1, 512]])),
                               (actt[b][:, 11:22], dap(act_scr, 11 * 128 * S_LEN + tt * 512,
                                                       [[S_LEN, 128], [128 * S_LEN, 11], [1, 512]]))],
                        acttB[b], actscrB, [acttB[b], lateB])
                    return
                DMA("sp", [(actt[b][:, 0:11], dap(act_scr, tt * 512, [[S_LEN, 128], [128 * S_LEN, 11], [1, 512]])),
                           (actt[b][:, 11:22], dap(act_scr, 11 * 128 * S_LEN + tt * 512,
                                                   [[S_LEN, 128], [128 * S_LEN, 11], [1, 512]]))],
                    acttB[b], actscrB, [acttB[b]])

            def loadH(i):
                if i < 32:
                    DMA("sp", [(ht[i % 2][:], h_scr.ap()[i * 128:(i + 1) * 128, :])], htB[i % 2], [hscrB], [htB[i % 2]])

            loadAct(0)
            loadH(0)
            loadH(1)
            for tt in range(8):
                b = tt % 2
                if tt + 1 < 8:
                    loadAct(tt + 1)
                for j in range(4):
                    i = tt * 4 + j
                    k = i % 2
                    hb = (i % 4) * 2
                    for half in range(2):
                        prs = [(actt[b][:, kc, j * 128:(j + 1) * 128], wd[:, kc, half * 512:(half + 1) * 512])
                               for kc in range(22)]
                        if tt == 0:
                            MM(bank(hb + half), prs[:NPRE], [acttB[b], wdB], [PB[hb + half]], start=True, stop=False)
                            MM(bank(hb + half), prs[NPRE:], [lateB, wdB], [PB[hb + half]], start=False, stop=True,
                               multi=True)
                        else:
                            MM(bank(hb + half), prs, [acttB[b], wdB], [PB[hb + half]])
                    TT(h2[k][:], ps[:, hb * 512:(hb + 2) * 512], ht[k][:], ALU.add,
                       [PB[hb], PB[hb + 1], htB[k]], [h2B[k]])
                    loadH(i + 2)
                    ACT(junk[:], h2[k][:], AF.Square, [h2B[k]], [junkB, ssqB[k]], accum=ssq[k][:])
                    ACT(std[k][:], ssq[k][:], AF.Sqrt, [ssqB[k], constB], [stdB[k]], bias=eps6[:], scale=1.0 / D)
                    RECIP(rstd[k][:], std[k][:], [stdB[k]], [rstdB[k]])
                    STT(ot[k][:], h2[k][:], rstd[k][:], gF[:], ALU.mult, ALU.mult, [h2B[k], rstdB[k], gFB], [otB[k]])
                    DMA("sp", [(out_d[i * 128:(i + 1) * 128, :], ot[k][:])], otB[k], [otB[k]], [])
            S.drain_dmas()
            S.emit()
        stW4.close()
    return nc


def _prep_weights(inp):
    f = np.float32
    w_in = np.asarray(inp["w_in"], f)[0]

    def chunked(w):
        n = w.shape[1] // 128
        return np.ascontiguousarray(w.reshape(8, 128, n, 128).transpose(2, 1, 0, 3))

    def rowmaj(w):
        k = w.shape[0] // 128
        return np.ascontiguousarray(w.reshape(k, 128, w.shape[1]).transpose(1, 0, 2))

    def v128(v):
        return np.ascontiguousarray(np.asarray(v, f).reshape(-1, 128).T)

    w_up = np.asarray(inp["w_up"], f)[0]
    bg = np.asarray(inp["b_gate"], f)[0]
    cw = np.asarray(inp["conv_dw_w"], f)[0]
    fw = np.asarray(inp["ffn_dw_w"], f)[0]
    vecs = np.zeros((128, V_N), f)
    vecs[:, V_G1:V_G1 + 8] = v128(inp["norm_mix_g"][0])
    vecs[:, V_BGA:V_BGA + 8] = v128(bg[:1024])
    vecs[:, V_BGB:V_BGB + 8] = v128(bg[1024:])
    vecs[:, V_CB:V_CB + 8] = v128(inp["conv_dw_b"][0])
    vecs[:, V_LNG:V_LNG + 8] = v128(inp["conv_ln_g"][0])
    vecs[:, V_LNB:V_LNB + 8] = v128(inp["conv_ln_b"][0])
    vecs[:, V_G2:V_G2 + 8] = v128(inp["norm_ffn_g"][0])
    vecs[:, V_FB:V_FB + 44] = v128(inp["ffn_dw_b"][0])
    vecs[:, V_CW:V_CW + 248] = cw.reshape(31, 8, 128).transpose(2, 1, 0).reshape(128, 248)
    vecs[:, V_FW:V_FW + 132] = fw.reshape(3, 44, 128).transpose(2, 1, 0).reshape(128, 132)
    return {
        "w_a": chunked(w_in[:, 0:1024]),
        "w_g": chunked(w_in[:, 1024:2048]),
        "w_q": chunked(w_in[:, 2048:3584]),
        "w_k": chunked(w_in[:, 3584:5120]),
        "w_v": rowmaj(w_in[:, 5120:6656]),
        "w_gA": rowmaj(w_in[:, 6656:7680]),
        "w_gB": rowmaj(w_in[:, 7680:8704]),
        "w_co": rowmaj(np.asarray(inp["w_conv_out"], f)[0]),
        "w_ao": rowmaj(np.asarray(inp["w_attn_out"], f)[0]),
        "w_o": rowmaj(np.asarray(inp["w_out"], f)[0]),
        "w_ua": chunked(w_up[:, :2816]),
        "w_uv": chunked(w_up[:, 2816:]),
        "w_d": rowmaj(np.asarray(inp["w_down"], f)[0]),
        "vecs": vecs,
        "gF": np.ascontiguousarray(np.asarray(inp["norm_final_g"], f).reshape(1, D)),
    }


def kernel(**inputs):
    x = np.asarray(inputs["x"], np.float32)
    wts = _prep_weights(inputs)
    nc = build()
    in_maps = []
    for c in range(NCORES):
        m = dict(wts)
        m["x"] = np.ascontiguousarray(x[c])
        in_maps.append(m)
    res = run_bass_kernel_spmd(nc, in_maps, core_ids=list(range(NCORES)))
    return np.stack([np.asarray(r["out"], np.float32).reshape(S_LEN, D) for r in res.results], axis=0)
```
